# Optimizing a Trainium2 kernel written in Bass

```python
import jax, jax.numpy as jnp
from jax import lax
import numpy as np

D_MODEL = 2048
BATCH = 16
SEQ = 2048
DEPTH = 2

WIDTH_POOL = D_MODEL // 4
WIDTH_CONV = (3 * D_MODEL) // 8
WIDTH_SGU = D_MODEL - WIDTH_POOL - WIDTH_CONV
POOL_WINDOWS = (2, 4, 8, 16)
N_POOL_GROUPS = len(POOL_WINDOWS)
POOL_GROUP_DIM = WIDTH_POOL // N_POOL_GROUPS
CONV_WIDTH = 31
N_SGU_HEADS = 6
SGU_HEAD_DIM = WIDTH_SGU // N_SGU_HEADS
CHUNK = 128
D_FF = 4 * D_MODEL
IN_WIDTH = WIDTH_POOL + 2 * WIDTH_CONV + 2 * WIDTH_SGU
N_MOD = 6
LN_EPS = 1e-5
ALPHA = (2.0 * DEPTH) ** 0.25
BETA = (8.0 * DEPTH) ** -0.25
MOD_INIT = 0.5

kernel_name = 'hybrid_pool_conv_sgu_deepnorm_block'


def _layer_norm(x, g, b):
    xf = x.astype(jnp.float32)
    mu = jnp.mean(xf, axis=-1, keepdims=True)
    var = jnp.mean(jnp.square(xf - mu), axis=-1, keepdims=True)
    y = (xf - mu) * lax.rsqrt(var + LN_EPS)
    return (y * g.astype(jnp.float32) + b.astype(jnp.float32)).astype(x.dtype)


def _causal_mean_minus_self(a, window):
    S = a.shape[1]
    af = a.astype(jnp.float32)
    cs = jnp.cumsum(af, axis=1)
    lagged = jnp.pad(cs, ((0, 0), (window, 0), (0, 0)))[:, :S]
    count = jnp.minimum(jnp.arange(1, S + 1, dtype=jnp.float32), float(window))
    mean = (cs - lagged) / count[None, :, None]
    return (mean - af).astype(a.dtype)


def _pool_mixer(a, w_grp, ls):
    B, S, _ = a.shape
    a = a.reshape(B, S, N_POOL_GROUPS, POOL_GROUP_DIM)
    pooled = jnp.stack([_causal_mean_minus_self(a[:, :, g], w) for g, w in enumerate(POOL_WINDOWS)], axis=2)
    y = jnp.einsum('bsgc,gcd->bsgd', pooled, w_grp)
    return y.reshape(B, S, WIDTH_POOL) * ls


def _conv_module(val, gate, conv_w, conv_b, ln_g, ln_b):
    h = val * jax.nn.sigmoid(gate)
    h = lax.conv_general_dilated(
        h, conv_w[:, None, :].astype(h.dtype), window_strides=(1,),
        padding=[(CONV_WIDTH - 1, 0)],
        dimension_numbers=('NWC', 'WIO', 'NWC'),
        feature_group_count=WIDTH_CONV) + conv_b
    h = _layer_norm(h, ln_g, ln_b)
    return jax.nn.silu(h)


def _sgu(z, w_s, b_s, ln_g, ln_b):
    z = jax.nn.gelu(z, approximate=False)
    u, v = jnp.split(z, 2, axis=-1)
    v = _layer_norm(v, ln_g, ln_b)
    B, S, _ = v.shape
    n_chunks = S // CHUNK
    v = v.reshape(B, n_chunks, CHUNK, N_SGU_HEADS, SGU_HEAD_DIM)
    mask = jnp.tril(jnp.ones((CHUNK, CHUNK), dtype=w_s.dtype))
    v = jnp.einsum('hts,bnshd->bnthd', w_s * mask, v) + b_s.T[:, :, None]
    return u * v.reshape(B, S, WIDTH_SGU)


def setup_inputs(seed: int = 0) -> dict:
    key = jax.random.key(seed)
    ks = jax.random.split(key, 24)
    f32 = jnp.float32
    nrm = lambda k, shape, s: jax.random.normal(k, shape, f32) * s
    L, D = DEPTH, D_MODEL
    return {
        'x': nrm(ks[0], (BATCH, SEQ, D), 1.0),
        'c': nrm(ks[1], (BATCH, D), 1.0),
        'w_mod': nrm(ks[2], (L, D, N_MOD * D), MOD_INIT * D ** -0.5),
        'b_mod': nrm(ks[3], (L, N_MOD * D), 0.02),
        'w_in': nrm(ks[4], (L, D, IN_WIDTH), D ** -0.5),
        'w_pool': nrm(ks[5], (L, N_POOL_GROUPS, POOL_GROUP_DIM, POOL_GROUP_DIM), POOL_GROUP_DIM ** -0.5),
        'ls_pool': 1.0 + nrm(ks[6], (L, WIDTH_POOL), 0.02),
        'conv_w': nrm(ks[7], (L, CONV_WIDTH, WIDTH_CONV), CONV_WIDTH ** -0.5),
        'conv_b': nrm(ks[8], (L, WIDTH_CONV), 0.02),
        'ln_conv_g': 1.0 + nrm(ks[9], (L, WIDTH_CONV), 0.02),
        'ln_conv_b': nrm(ks[10], (L, WIDTH_CONV), 0.02),
        'w_sgu': nrm(ks[11], (L, N_SGU_HEADS, CHUNK, CHUNK), CHUNK ** -0.5),
        'b_sgu': 1.0 + nrm(ks[12], (L, N_SGU_HEADS, CHUNK), 0.02),
        'ln_sgu_g': 1.0 + nrm(ks[13], (L, WIDTH_SGU), 0.02),
        'ln_sgu_b': nrm(ks[14], (L, WIDTH_SGU), 0.02),
        'w_out': nrm(ks[15], (L, D, D), BETA * D ** -0.5),
        'ln_mix_g': 1.0 + nrm(ks[16], (L, D), 0.02),
        'ln_mix_b': nrm(ks[17], (L, D), 0.02),
        'w_ff1': nrm(ks[18], (L, D, D_FF), D ** -0.5),
        'w_ff2': nrm(ks[19], (L, D_FF, D), BETA * D_FF ** -0.5),
        'ln_ff_g': 1.0 + nrm(ks[20], (L, D), 0.02),
        'ln_ff_b': nrm(ks[21], (L, D), 0.02),
    }


def reference(x, c, w_mod, b_mod, w_in, w_pool, ls_pool, conv_w, conv_b, ln_conv_g, ln_conv_b,
              w_sgu, b_sgu, ln_sgu_g, ln_sgu_b, w_out, ln_mix_g, ln_mix_b,
              w_ff1, w_ff2, ln_ff_g, ln_ff_b):
    c_act = jax.nn.silu(c)
    split_at = [WIDTH_POOL, WIDTH_POOL + WIDTH_CONV, WIDTH_POOL + 2 * WIDTH_CONV]
    for l in range(DEPTH):
        mod = c_act @ w_mod[l] + b_mod[l]
        sh_m, sc_m, g_m, sh_f, sc_f, g_f = [m[:, None, :] for m in jnp.split(mod, N_MOD, axis=-1)]

        h = x * (1.0 + sc_m) + sh_m
        p = h @ w_in[l]
        a, val, gate, z = jnp.split(p, split_at, axis=-1)
        y_pool = _pool_mixer(a, w_pool[l], ls_pool[l])
        y_conv = _conv_module(val, gate, conv_w[l], conv_b[l], ln_conv_g[l], ln_conv_b[l])
        y_sgu = _sgu(z, w_sgu[l], b_sgu[l], ln_sgu_g[l], ln_sgu_b[l])
        y = jnp.concatenate([y_pool, y_conv, y_sgu], axis=-1) @ w_out[l]
        x = _layer_norm(ALPHA * x + g_m * y, ln_mix_g[l], ln_mix_b[l])

        h = x * (1.0 + sc_f) + sh_f
        f = jnp.square(jax.nn.relu(h @ w_ff1[l])) @ w_ff2[l]
        x = _layer_norm(ALPHA * x + g_f * f, ln_ff_g[l], ln_ff_b[l])
    return x
```

```python
import numpy as np
import concourse.bass as bass
import concourse.mybir as mybir
from concourse.bass_utils import run_bass_kernel_spmd
from contextlib import ExitStack

F32 = mybir.dt.float32
BF16 = mybir.dt.bfloat16
U8 = mybir.dt.uint8
I32 = mybir.dt.int32
F32R = mybir.dt.float32r
AF = mybir.ActivationFunctionType
ALU = mybir.AluOpType

D = 2048
KC = 16
T = 512
L_ALL = 2
SEQ = 2048
NSLOT = 44
SLOTE = 8192
ALPHA = (2.0 * L_ALL) ** 0.25
EPS = 1e-5
EPSM = EPS / (ALPHA * ALPHA)
NCORES = 8
NGC = 192


class Sem:
    def __init__(self, h):
        self.h = h
        self.count = 0


class Region:
    _all = {}

    def __init__(self, space, lo, hi, name=""):
        self.space, self.lo, self.hi, self.name = space, lo, hi, name
        self.w = None
        self.r = {}
        self._ov = None
        self._ver = -1
        Region._all.setdefault(space, []).append(self)

    def overlaps(self):
        lst = Region._all[self.space]
        if self._ver != len(lst):
            self._ov = [o for o in lst if o.lo < self.hi and self.lo < o.hi]
            self._ver = len(lst)
        return self._ov


class Eng:
    def __init__(self, h, sem, is_pe=False):
        self.h, self.sem, self.is_pe = h, sem, is_pe
        self.known = {}


class K:
    def __init__(self, nc, es):
        self.nc, self.es = nc, es
        self.eng = {}
        for name, h in (("pe", nc.tensor), ("act", nc.scalar), ("dve", nc.vector),
                        ("pool", nc.gpsimd), ("sp", nc.sync)):
            self.eng[name] = Eng(h, self.newsem("e_" + name), is_pe=(name == "pe"))

    def newsem(self, name):
        return Sem(self.es.enter_context(self.nc.semaphore(name)))

    def _waits(self, E, reads, writes):
        deps = {}

        def add(tk):
            s, v = tk
            if deps.get(s, 0) < v:
                deps[s] = v
        for R in reads:
            for O in R.overlaps():
                if O.w is not None:
                    add(O.w)
        for R in writes:
            for O in R.overlaps():
                if O.w is not None:
                    add(O.w)
                for tk in O.r.items():
                    add(tk)
        for s, v in deps.items():
            if s is E.sem and E.is_pe:
                continue
            if E.known.get(s, 0) >= v:
                continue
            E.h.wait_ge(s.h, v)
            E.known[s] = v

    def op(self, eng, fn, reads=(), writes=(), inc=True):
        E = self.eng[eng]
        self._waits(E, reads, writes)
        ins = fn()
        v = E.sem.count + 1
        if inc:
            ins.then_inc(E.sem.h, 1)
            E.sem.count += 1
        for R in reads:
            if R.r.get(E.sem, 0) < v:
                R.r[E.sem] = v
        for R in writes:
            R.w = (E.sem, v)
            R.r = {}
        return ins

    def dma(self, q, out, in_, reads, writes, sem, **kw):
        E = self.eng[q]
        self._waits(E, reads, writes)
        ins = E.h.dma_start(out=out, in_=in_, **kw)
        ins.then_inc(sem.h, 16)
        sem.count += 16
        for R in reads:
            R.r[sem] = sem.count
        for R in writes:
            R.w = (sem, sem.count)
            R.r = {}
        return ins


class Buf:
    def __init__(self, arena_ap, lo, nbytes, name):
        self.a, self.lo, self.nbytes, self.name = arena_ap, lo, nbytes, name
        self.reg = Region("sb", lo, lo + nbytes, name)
        self._subs = {}

    def v(self, dt, lo=0, n=None):
        esz = 4 if dt in (F32, I32, F32R) else (2 if dt == BF16 else 1)
        if n is None:
            n = (self.nbytes - lo) // esz
        return self.a[:, self.lo + lo:self.lo + lo + n * esz].bitcast(dt)

    def sub(self, lo, nbytes):
        key = (lo, nbytes)
        if key not in self._subs:
            self._subs[key] = Region("sb", self.lo + lo, self.lo + lo + nbytes, f"{self.name}[{lo}]")
        return self._subs[key]


def build_program(nt, layers, stop_after=None):
    n_cores, gather = 1, False
    nc = bass.Bass("TRN2", target_bir_lowering=False)
    ntok = nt * T
    nloc_slots = (L_ALL * NSLOT) // n_cores
    nloc_gc = NGC // n_cores
    nloc_grp = nloc_gc // 4

    def dram(name, shape, dt, kind):
        return nc.dram_tensor(name, shape, dt, kind=kind).ap()
    x_d = dram("x", [ntok, D], F32, "ExternalInput")
    y_d = dram("y", [ntok, D], F32, "ExternalOutput")
    wsl_d = dram("wsl", [nloc_slots * 128, SLOTE], F32, "ExternalInput")
    wmod_d = dram("wmod", [NGC // 4 * 128, 8192], F32, "ExternalInput")
    bmod_d = dram("bmod", [128, NGC], F32, "ExternalInput")
    c2_d = dram("c2", [128, 32], F32, "ExternalInput")
    NPP = L_ALL * (6 * 31 + 6 * 3 + 4 + 16 * 4)
    pp_d = dram("pp", [128, NPP], F32, "ExternalInput")
    sgb_d = dram("sgb", [1, L_ALL * 2 * 768], F32, "ExternalInput")
    bsr_d = dram("bsr", [1, L_ALL * 6 * 128], F32, "ExternalInput")
    wp_d = dram("wp", [128, L_ALL * 4 * 128], F32, "ExternalInput")
    wst_d = dram("wst", [128, L_ALL * 6 * 128], F32, "ExternalInput")
    wbfull_d = dram("wbfull", [L_ALL * NSLOT * 128, SLOTE], BF16, "Internal")
    wbsh_d = wbfull_d

    Region._all = {}
    es = ExitStack()
    with es:
        k = K(nc, es)
        ARENA = 212800
        arena_t = es.enter_context(nc.sbuf_tensor("arena", [128, ARENA], U8))
        arena = arena_t[:, :]
        psum_t = es.enter_context(nc.psum_tensor("ps", [128, 8 * 512], F32))
        psum = psum_t[:, :]
        banks = [psum[:, i * 512:(i + 1) * 512] for i in range(8)]
        bank_r = [Region("ps", i * 2048, (i + 1) * 2048, f"bank{i}") for i in range(8)]
        rr = [0]

        def nextbank():
            i = rr[0] % 6
            rr[0] += 1
            return banks[i], bank_r[i]
        SUMB, SQB = 6, 7

        cur = [0]

        def alloc(name, nbytes, at=None):
            if at is None:
                lo = (cur[0] + 63) // 64 * 64
                cur[0] = lo + nbytes
                assert cur[0] <= ARENA, (name, cur[0])
            else:
                lo = at
            return Buf(arena, lo, nbytes, name)

        XT = alloc("xT", 32768)
        HT = alloc("hT", 16384)
        U0 = (cur[0] + 63) // 64 * 64
        cur[0] = U0 + 65536
        WR = [alloc(f"wr{i}", 16384) for i in range(3)]
        uo = [U0]

        def ualloc(name, nbytes):
            lo = (uo[0] + 63) // 64 * 64
            uo[0] = lo + nbytes
            assert uo[0] <= U0 + 65536, (name, uo[0] - U0)
            return Buf(arena, lo, nbytes, name)
        YT = ualloc("yT", 16384)
        AT = ualloc("aT", 4 * 528 * 4)
        HC = ualloc("hcT", 6 * 544 * 2)
        CV = ualloc("cvT", 6 * 512 * 2)
        UT = ualloc("uT", 6 * 512 * 2)
        VLN = ualloc("vln", 4 * 768 * 2)
        VG = ualloc("vg", 768 * 4)
        PT = ualloc("ptmp", 2 * 528 * 4)
        DG = ualloc("diag", 31 * 128 * 2)
        H1 = Buf(arena, U0, 65536, "h1T")
        DG2 = Buf(arena, AT.lo, 31 * 128 * 2, "diag2")
        STG = [Buf(arena, U0 + i * 8192, 8192, f"stg{i}") for i in range(2)]
        WMB = [Buf(arena, U0 + i * 32768, 32768, f"wmb{i}") for i in range(2)]
        MF = Buf(arena, XT.lo, NGC * 16 * 4, "mf")
        WP32 = Buf(arena, XT.lo + 12288, L_ALL * 4 * 128 * 4, "wp32")
        WST32 = Buf(arena, XT.lo + 16384, L_ALL * 6 * 128 * 4, "wst32")
        ONESF = Buf(arena, XT.lo + 22528, 512, "onesf")
        IOTA = Buf(arena, XT.lo + 23040, 64, "iota")
        IOTF = Buf(arena, XT.lo + 23104, 64, "iotf")
        MSEL = Buf(arena, XT.lo + 23168, NGC * 16 * 4, "msel")
        RB = alloc("rb", 3 * 1024)
        RSQ = alloc("rsq", 3 * 1024)
        MEAN = alloc("mean", 2048)
        RSTD = alloc("rstd", 2048)
        TMPA = alloc("tmpa", 2048)
        FT = alloc("ft", 2 * 2048)
        SQT = alloc("sqt", 2 * 1024)
        ID32 = alloc("id32", 512)
        IDB = alloc("idb", 256)
        ONEB = alloc("oneb", 256)
        ONER = alloc("oner", 512)
        WMT = alloc("wmt", L_ALL * 6 * 128 * 2)
        WPB = alloc("wpb", L_ALL * 4 * 128 * 2)
        SGG = alloc("sgg", L_ALL * 768 * 4)
        SGBB = alloc("sgbb", L_ALL * 768 * 2)
        SGB32 = Buf(arena, XT.lo + 40960, L_ALL * 768 * 4, "sgb32")
        BSR = alloc("bsr", L_ALL * 6 * 128 * 4)
        PP = alloc("pp", NPP * 4)
        MOD2 = alloc("mod2", NGC * 2 * 4)
        DER = alloc("der", 9 * L_ALL * 32 * 4)
        INVC = alloc("invc", 4 * 16 * 4)
        AHALO = alloc("ahalo", L_ALL * 4 * 16 * 4)
        HHALO = alloc("hhalo", L_ALL * 6 * 32 * 2)
        SMALL = alloc("small", 256)
        CACT = Buf(arena, XT.lo + 36864, 256 * 4, "cact")
        SEL = Buf(arena, XT.lo + 39552, 32 * 4, "sel")
        EPSB = alloc("eps", 16)
        if nloc_gc <= 32:
            MODL = Buf(arena, XT.lo + 37888, nloc_gc * 16 * 4, "modl")
            BMOD = Buf(arena, XT.lo + 39424, nloc_gc * 4, "bmod")
        else:
            MODL = Buf(arena, WR[0].lo, NGC * 16 * 4, "modl")
            BMOD = Buf(arena, WR[1].lo, NGC * 4, "bmod")

        ppo = {}
        o = 0
        for nm, n in (("convw", 6 * 31), ("convb", 6), ("lncg", 6), ("lncb", 6), ("ls", 4),
                      ("lnmg", 16), ("lnmb", 16), ("lnfg", 16), ("lnfb", 16)):
            ppo[nm] = (o, n)
            o += L_ALL * n
        assert o == NPP

        def ppv(nm, l, i0=0, n=1):
            o, per = ppo[nm]
            a = o + l * per + i0
            return PP.v(F32)[:, a:a + n]

        csem = k.newsem("csem")
        castsem = k.newsem("castsem")
        gsem = k.newsem("gsem")
        msem = k.newsem("msem")
        wmsem = [k.newsem(f"wmsem{i}") for i in range(2)]
        ringsem = [k.newsem(f"ring{i}") for i in range(3)]
        stgsem = [k.newsem(f"stgs{i}") for i in range(2)]
        osem = [k.newsem(f"osem{i}") for i in range(2)]

        R_wbsh = Region("d_wbsh", 0, 1, "wbsh")
        R_wbfull = Region("d_wbfull", 0, 1, "wbfull") if gather else R_wbsh
        R_modsh = Region("d_modsh", 0, 1, "modsh")
        R_modfull = Region("d_modfull", 0, 1, "modfull") if gather else R_modsh
        R_y = Region("d_y", 0, 1, "y")

        C2 = Buf(arena, XT.lo + 36864, 32 * 4, "c2")
        BMD = Buf(arena, XT.lo + 37888, NGC * 4, "bmd")
        for buf, src in ((PP, pp_d), (C2, c2_d), (BMD, bmod_d), (WP32, wp_d), (WST32, wst_d)):
            k.dma("sp", buf.v(F32), src, [], [buf.reg], csem)
        k.dma("sp", BSR.v(F32)[0:1, :], bsr_d, [], [BSR.reg], csem)
        k.dma("sp", SGG.v(F32), sgb_d[:, 0:L_ALL * 768].partition_broadcast(128), [], [SGG.reg], csem)
        k.dma("sp", SGB32.v(F32), sgb_d[:, L_ALL * 768:2 * L_ALL * 768].partition_broadcast(128), [], [SGB32.reg], csem)
        for b_ in (PP, C2, BMD, WP32, WST32, BSR, SGG, SGB32):
            b_.reg.w = (csem, csem.count)

        R_slot = {g: Region("d_wb", g, g + 1, f"wb{g}") for g in range(L_ALL * NSLOT)}
        stsem = [k.newsem(f"wst{i}") for i in range(3)]
        ringsem_sw = [k.newsem(f"ringsw{i}") for i in range(3)]

        k.op("dve", lambda: nc.vector.memset(EPSB.v(F32)[:, 0:1], EPS), [], [EPSB.reg])
        k.op("dve", lambda: nc.vector.memset(EPSB.v(F32)[:, 1:2], EPSM), [EPSB.reg], [EPSB.reg])
        k.op("pool", lambda: nc.gpsimd.memset(ONESF.v(F32), 1.0), [], [ONESF.reg])
        k.op("pool", lambda: nc.gpsimd.memset(ONEB.v(BF16), 1.0), [], [ONEB.reg])
        k.op("pool", lambda: nc.gpsimd.memset(ONER.v(F32), 1.0), [], [ONER.reg])
        k.op("pool", lambda: nc.gpsimd.affine_select(
            out=ID32.v(F32), in_=ONESF.v(F32), pattern=[[1, 128]], compare_op=ALU.is_equal,
            fill=0.0, base=0, channel_multiplier=-1), [ONESF.reg], [ID32.reg])
        k.op("pool", lambda: nc.gpsimd.tensor_copy(out=IDB.v(BF16), in_=ID32.v(F32)), [ID32.reg], [IDB.reg])
        k.op("pool", lambda: nc.gpsimd.affine_select(
            out=WMT.v(BF16).rearrange("p (a t) -> p a t", t=128),
            in_=WST32.v(F32).rearrange("p (a t) -> p a t", t=128),
            pattern=[[0, L_ALL * 6], [1, 128]], compare_op=ALU.is_ge,
            fill=0.0, base=0, channel_multiplier=-1), [WST32.reg], [WMT.reg])
        k.op("pool", lambda: nc.gpsimd.tensor_copy(out=WPB.v(BF16), in_=WP32.v(F32)), [WP32.reg], [WPB.reg])
        k.op("pool", lambda: nc.gpsimd.tensor_copy(out=SGBB.v(BF16), in_=SGB32.v(F32)), [SGB32.reg], [SGBB.reg])
        k.op("pool", lambda: nc.gpsimd.iota(IOTA.v(I32), pattern=[[1, 16]], base=1, channel_multiplier=0),
             [], [IOTA.reg])
        k.op("dve", lambda: nc.vector.tensor_copy(out=IOTF.v(F32), in_=IOTA.v(I32)), [IOTA.reg], [IOTF.reg])
        for g in range(4):
            k.op("dve", lambda g=g: nc.vector.tensor_scalar(
                out=INVC.v(F32)[:, g * 16:(g + 1) * 16], in0=IOTF.v(F32), scalar1=float(2 << g), scalar2=None,
                op0=ALU.min), [IOTF.reg, INVC.reg], [INVC.reg])
        k.op("dve", lambda: nc.vector.reciprocal(out=INVC.v(F32), in_=INVC.v(F32)), [INVC.reg], [INVC.reg])

        id32 = ID32.v(F32)
        k.op("act", lambda: nc.scalar.activation(out=C2.v(F32), in_=C2.v(F32), func=AF.Silu), [C2.reg], [C2.reg])
        c2v = C2.v(F32).rearrange("p (k b) -> p k b", b=2)
        modps = banks[SUMB]
        MROW = [Buf(arena, XT.lo + 47104 + i * 2048, 2048, f"mrow{i}") for i in range(2)]
        for grp in range(NGC // 4):
            wb_ = WMB[grp % 2]
            k.dma("sp", wb_.v(F32), wmod_d[grp * 128:(grp + 1) * 128, :], [], [wb_.reg], wmsem[grp % 2])
            wv = wb_.v(F32)
            bank, bankr = nextbank()
            for kc in range(KC):
                k.op("pe", lambda kc=kc, wv=wv, bank=bank: nc.tensor.matmul(
                    bank[0:2, :], lhsT=c2v[:, kc, :], rhs=wv[:, kc * 512:(kc + 1) * 512],
                    start=(kc == 0), stop=(kc == KC - 1)), [wb_.reg, C2.reg], [bankr], inc=(kc == KC - 1))
            mr = MROW[grp % 2]
            k.op("act", lambda bank=bank, mr=mr: nc.scalar.activation(out=mr.v(F32)[0:2, :], in_=bank[0:2, :],
                                                                     func=AF.Copy), [bankr], [mr.reg])
            for c4 in range(4):
                gc = grp * 4 + c4
                k.op("pe", lambda c4=c4, gc=gc, mr=mr: nc.tensor.matmul(
                    modps[:, gc * 2:(gc + 1) * 2], lhsT=mr.v(F32)[0:2, c4 * 128:(c4 + 1) * 128],
                    rhs=id32[0:2, 0:2], start=True, stop=True), [mr.reg, ID32.reg], [bank_r[SUMB]], inc=(c4 == 3))
        k.op("dve", lambda: nc.vector.tensor_tensor(
            out=MOD2.v(F32).rearrange("p (c b) -> p c b", b=2),
            in0=modps[:, 0:NGC * 2].rearrange("p (c b) -> p c b", b=2),
            in1=BMD.v(F32).unsqueeze(2).to_broadcast([128, NGC, 2]),
            op=ALU.add), [bank_r[SUMB], BMD.reg], [MOD2.reg])
        mod2 = MOD2.v(F32)

        def modv(l, m, kc, b):
            gc = l * 96 + m * 16 + kc
            return mod2[:, gc * 2 + b:gc * 2 + b + 1]

        def mod_all(l, m):
            gc = l * 96 + m * 16
            return mod2[:, gc * 2:(gc + 16) * 2].rearrange("p (k b) -> p k b", b=2)
        der = DER.v(F32)

        def dv(idx, l):
            a = (idx * L_ALL + l) * 32
            return der[:, a:a + 32].rearrange("p (k b) -> p k b", b=2)

        def dsc(idx, l, kc, b):
            a = (idx * L_ALL + l) * 32 + kc * 2 + b
            return der[:, a:a + 1]
        S1M, GMA, GFA, S1F, A1, B1, A2, B2, SHM = range(9)
        DR = [DER.reg, MOD2.reg, PP.reg]

        def ppb(nm, l):
            o, per = ppo[nm]
            return PP.v(F32)[:, o + l * per:o + l * per + 16].unsqueeze(2).to_broadcast([128, 16, 2])
        for l in range(L_ALL):
            k.op("dve", lambda l=l: nc.vector.tensor_scalar(out=dv(S1M, l), in0=mod_all(l, 1), scalar1=1.0,
                                                           scalar2=None, op0=ALU.add), DR, [DER.reg])
            k.op("dve", lambda l=l: nc.vector.tensor_scalar(out=dv(S1F, l), in0=mod_all(l, 4), scalar1=1.0,
                                                           scalar2=None, op0=ALU.add), DR, [DER.reg])
            k.op("dve", lambda l=l: nc.vector.tensor_scalar(out=dv(GMA, l), in0=mod_all(l, 2), scalar1=1.0 / ALPHA,
                                                           scalar2=None, op0=ALU.mult), DR, [DER.reg])
            k.op("dve", lambda l=l: nc.vector.tensor_scalar(out=dv(GFA, l), in0=mod_all(l, 5), scalar1=1.0 / ALPHA,
                                                           scalar2=None, op0=ALU.mult), DR, [DER.reg])
            k.op("dve", lambda l=l: nc.vector.tensor_copy(out=dv(SHM, l), in_=mod_all(l, 0)), DR, [DER.reg])
        for l in range(L_ALL):
            k.op("dve", lambda l=l: nc.vector.tensor_tensor(out=dv(A1, l), in0=dv(S1F, l), in1=ppb("lnmg", l),
                                                           op=ALU.mult), DR, [DER.reg])
            k.op("dve", lambda l=l: nc.vector.tensor_tensor(out=dv(B1, l), in0=dv(S1F, l), in1=ppb("lnmb", l),
                                                           op=ALU.mult), DR, [DER.reg])
            k.op("dve", lambda l=l: nc.vector.tensor_tensor(out=dv(B1, l), in0=dv(B1, l), in1=mod_all(l, 3),
                                                           op=ALU.add), DR, [DER.reg])
            if l + 1 < L_ALL:
                k.op("dve", lambda l=l: nc.vector.tensor_tensor(out=dv(A2, l), in0=dv(S1M, l + 1),
                                                               in1=ppb("lnfg", l), op=ALU.mult), DR, [DER.reg])
                k.op("dve", lambda l=l: nc.vector.tensor_tensor(out=dv(B2, l), in0=dv(S1M, l + 1),
                                                               in1=ppb("lnfb", l), op=ALU.mult), DR, [DER.reg])
                k.op("dve", lambda l=l: nc.vector.tensor_tensor(out=dv(B2, l), in0=dv(B2, l), in1=mod_all(l + 1, 0),
                                                               op=ALU.add), DR, [DER.reg])

        order = [(l, s) for _ in range(nt) for l in layers for s in range(NSLOT)]
        wst = {"issued": 0, "taken": 0}

        def w_issue():
            i = wst["issued"]
            if i >= len(order):
                return
            l, s = order[i]
            g = l * NSLOT + s
            rb_ = WR[i % 3]
            rows = slice(g * 128, (g + 1) * 128)
            if i < NSLOT * len(layers):
                k.dma("pool", rb_.v(BF16), wsl_d[rows, :], [], [rb_.reg], ringsem_sw[i % 3], max_dma_last_dim=8192)
                if nt > 1:
                    k.dma("sp", wbfull_d[rows, :], rb_.v(BF16), [rb_.reg], [R_slot[g]], stsem[i % 3])
            else:
                k.dma("sp", rb_.v(BF16), wbfull_d[rows, :], [R_slot[g]], [rb_.reg], ringsem[i % 3])
            wst["issued"] += 1

        def w_get(l, s, hold=0):
            i = wst["taken"]
            assert order[i] == (l, s), (order[i], l, s)
            while wst["issued"] <= min(i + 2 - hold, len(order) - 1):
                w_issue()
            wst["taken"] += 1
            return WR[i % 3]

        xT = XT.v(F32).rearrange("p (k t) -> p k t", t=T)
        hT = HT.v(BF16).rearrange("p (k t) -> p k t", t=T)
        yT = YT.v(BF16).rearrange("p (k t) -> p k t", t=T)
        h1T = H1.v(BF16).rearrange("p (k t) -> p k t", t=T)
        aT = AT.v(F32).rearrange("p (g t) -> p g t", t=528)
        hcT = HC.v(BF16).rearrange("p (c t) -> p c t", t=544)
        cvT = CV.v(BF16).rearrange("p (c t) -> p c t", t=T)
        uT = UT.v(BF16).rearrange("p (c t) -> p c t", t=T)
        vln = VLN.v(BF16).rearrange("p (b f) -> p b f", f=768)
        vg = VG.v(F32)
        ptmp = PT.v(F32).rearrange("p (i t) -> p i t", t=528)
        diags = [(b_.v(BF16).rearrange("p (k m) -> p k m", m=128), b_.reg) for b_ in (DG, DG2)]
        rb = RB.v(BF16).rearrange("p (i t) -> p i t", t=T)
        rsq = RSQ.v(BF16).rearrange("p (i t) -> p i t", t=T)
        ft = FT.v(F32).rearrange("p (i t) -> p i t", t=T)
        sqt = SQT.v(BF16).rearrange("p (i t) -> p i t", t=T)
        mean, rstd, tmpa = MEAN.v(F32), RSTD.v(F32), TMPA.v(F32)
        id32, idb, oneb = ID32.v(F32), IDB.v(BF16), ONEB.v(BF16)
        wmt = WMT.v(BF16).rearrange("p (a t) -> p a t", t=128)
        wpb = WPB.v(BF16).rearrange("p (a t) -> p a t", t=128)
        sgg = SGG.v(F32).rearrange("p (l f) -> p l f", f=768)
        sgbb = SGBB.v(BF16).rearrange("p (l f) -> p l f", f=768)
        bsr = BSR.v(F32)
        oner = ONER.v(F32)
        ahalo = AHALO.v(F32).rearrange("p (l g t) -> p l g t", g=4, t=16)
        hhalo = HHALO.v(BF16).rearrange("p (l c t) -> p l c t", c=6, t=32)
        small = SMALL.v(F32)
        epsb = EPSB.v(F32)

        def xr(kc):
            return XT.sub(kc * 2048, 2048)

        def hr(kc):
            return HT.sub(kc * 1024, 1024)

        def yr(kc):
            return YT.sub(kc * 1024, 1024)

        def h1r(j):
            return H1.sub(j * 1024, 1024)

        def cvr(c):
            return CV.sub(c * 1024, 1024)

        def ur(c):
            return UT.sub(c * 1024, 1024)

        def hcr(c):
            return HC.sub(c * 1088, 1088)

        def ar(g):
            return AT.sub(g * 2112, 2112)

        def ln_stats(nfeat, eps_col):
            inv = 1.0 / nfeat
            k.op("dve", lambda: nc.vector.tensor_scalar(out=mean, in0=banks[SUMB], scalar1=inv, scalar2=None,
                                                       op0=ALU.mult), [bank_r[SUMB]], [MEAN.reg])
            k.op("dve", lambda: nc.vector.tensor_tensor(out=tmpa, in0=mean, in1=mean, op=ALU.mult),
                 [MEAN.reg], [TMPA.reg])
            k.op("dve", lambda: nc.vector.scalar_tensor_tensor(out=tmpa, in0=banks[SQB], scalar=inv, in1=tmpa,
                                                              op0=ALU.mult, op1=ALU.subtract),
                 [bank_r[SQB], TMPA.reg], [TMPA.reg])
            k.op("act", lambda: nc.scalar.activation(out=rstd, in_=tmpa, func=AF.Sqrt,
                                                    bias=epsb[:, eps_col:eps_col + 1], scale=1.0),
                 [TMPA.reg, EPSB.reg], [RSTD.reg])
            k.op("dve", lambda: nc.vector.reciprocal(out=banks[SQB], in_=rstd), [RSTD.reg], [bank_r[SQB]])

        def main_ln(l, b, gidx, aidx, bidx, gname, bname, make_h, pending):
            for p_ in pending:
                p_()
            ln_stats(float(D), 1)
            for kc in range(KC):
                k.op("dve", lambda kc=kc: nc.vector.scalar_tensor_tensor(
                    out=xT[:, kc, :], in0=banks[SUMB], scalar=-1.0 / D, in1=xT[:, kc, :], op0=ALU.mult, op1=ALU.add),
                    [xr(kc), bank_r[SUMB]], [xr(kc)])
                k.op("dve", lambda kc=kc: nc.vector.tensor_tensor(out=xT[:, kc, :], in0=banks[SQB], in1=xT[:, kc, :],
                                                                 op=ALU.mult), [xr(kc), bank_r[SQB]], [xr(kc)])
                if make_h:
                    k.op("act", lambda kc=kc: nc.scalar.activation(
                        out=hT[:, kc, :], in_=xT[:, kc, :], func=AF.Identity,
                        scale=dsc(aidx, l, kc, b), bias=dsc(bidx, l, kc, b)), [xr(kc), DER.reg], [hr(kc)])
                k.op("act", lambda kc=kc: nc.scalar.activation(
                    out=xT[:, kc, :], in_=xT[:, kc, :], func=AF.Identity,
                    scale=ppv(gname, l, kc), bias=ppv(bname, l, kc)), [xr(kc), PP.reg], [xr(kc)])

        def resid_evac(l, b, gidx, oc, bank, bankr, pend):
            k.op("dve", lambda: nc.vector.scalar_tensor_tensor(
                out=xT[:, oc, :], in0=bank, scalar=dsc(gidx, l, oc, b), in1=xT[:, oc, :],
                op0=ALU.mult, op1=ALU.add), [bankr, xr(oc), DER.reg], [xr(oc)])
            i = oc % 3
            rbr, rsr = RB.sub(i * 1024, 1024), RSQ.sub(i * 1024, 1024)
            k.op("pool", lambda: nc.gpsimd.tensor_copy(out=rb[:, i, :], in_=xT[:, oc, :]), [xr(oc)], [rbr])
            k.op("pool", lambda: nc.gpsimd.tensor_tensor(out=rsq[:, i, :], in0=xT[:, oc, :], in1=xT[:, oc, :],
                                                        op=ALU.mult), [xr(oc)], [rsr])

            def stats():
                k.op("pe", lambda: nc.tensor.matmul(banks[SUMB], lhsT=oneb, rhs=rb[:, i, :], start=(oc == 0),
                                                   stop=(oc == KC - 1)), [ONEB.reg, rbr], [bank_r[SUMB]],
                     inc=(oc == KC - 1))
                k.op("pe", lambda: nc.tensor.matmul(banks[SQB], lhsT=oneb, rhs=rsq[:, i, :], start=(oc == 0),
                                                   stop=(oc == KC - 1)), [ONEB.reg, rsr], [bank_r[SQB]],
                     inc=(oc == KC - 1))
            pend.append(stats)
            if len(pend) > 2:
                pend.pop(0)()

        for ti in range(nt):
            tok0 = ti * T
            b = (tok0 // SEQ) % 2
            seq_start = (tok0 % SEQ == 0)
            for tb in range(4):
                st = STG[tb % 2]
                k.dma("sp", st.v(F32), x_d[tok0 + tb * 128:tok0 + (tb + 1) * 128, :], [], [st.reg], stgsem[tb % 2])
                sv = st.v(F32)
                for q in range(4):
                    bank, bankr = nextbank()
                    for j in range(4):
                        kc = q * 4 + j
                        k.op("pe", lambda kc=kc, j=j, bank=bank, sv=sv: nc.tensor.transpose(
                            bank[:, j * 128:(j + 1) * 128], sv[:, kc * 128:(kc + 1) * 128], id32),
                            [st.reg, ID32.reg], [bankr], inc=(j == 3))
                    k.op("dve", lambda q=q, tb=tb, bank=bank: nc.vector.tensor_copy(
                        out=xT[:, q * 4:(q + 1) * 4, tb * 128:(tb + 1) * 128],
                        in_=bank.rearrange("p (j t) -> p j t", t=128)),
                        [bankr], [xr(q * 4 + j_) for j_ in range(4)])
            for li, l in enumerate(layers):
                last = (li == len(layers) - 1)
                if li == 0:
                    for kc in range(KC):
                        k.op("act", lambda kc=kc: nc.scalar.activation(
                            out=hT[:, kc, :], in_=xT[:, kc, :], func=AF.Identity,
                            scale=dsc(S1M, l, kc, b), bias=dsc(SHM, l, kc, b)), [xr(kc), DER.reg], [hr(kc)])
                if seq_start:
                    k.op("pool", lambda: nc.gpsimd.memset(aT[:, :, 0:16], 0.0), [], [AT.reg])
                    k.op("pool", lambda: nc.gpsimd.memset(hcT[:, :, 0:32], 0.0), [], [HC.reg])
                else:
                    k.op("pool", lambda: nc.gpsimd.tensor_copy(out=aT[:, :, 0:16], in_=ahalo[:, l]),
                         [AHALO.reg], [AT.reg])
                    k.op("pool", lambda: nc.gpsimd.tensor_copy(out=hcT[:, :, 0:32], in_=hhalo[:, l]),
                         [HHALO.reg], [HC.reg])
                w6, w7 = w_get(l, 0), w_get(l, 1, hold=1)
                for tb in range(4):
                    for half, wz in enumerate((w6, w7)):
                        wv = wz.v(BF16)
                        bank, bankr = nextbank()
                        for kc in range(KC):
                            k.op("pe", lambda kc=kc, wv=wv, bank=bank, tb=tb: nc.tensor.matmul(
                                bank[:, 0:384], lhsT=hT[:, kc, tb * 128:(tb + 1) * 128],
                                rhs=wv[:, kc * 384:(kc + 1) * 384], start=(kc == 0), stop=(kc == KC - 1)),
                                [wz.reg, hr(kc)], [bankr], inc=(kc == KC - 1))
                        k.op("act", lambda half=half, bank=bank: nc.scalar.activation(
                            out=vg[:, half * 384:(half + 1) * 384], in_=bank[:, 0:384], func=AF.Gelu),
                            [bankr], [VG.reg])
                    SR = [SMALL.reg]
                    k.op("dve", lambda: nc.vector.bn_stats(out=small[:, 0:6], in_=vg[:, 0:384]), [VG.reg], SR)
                    k.op("dve", lambda: nc.vector.bn_stats(out=small[:, 6:12], in_=vg[:, 384:768]), [VG.reg] + SR, SR)
                    k.op("dve", lambda: nc.vector.bn_aggr(out=small[:, 12:14], in_=small[:, 0:12]), SR, SR)
                    k.op("act", lambda: nc.scalar.activation(out=small[:, 14:15], in_=small[:, 13:14], func=AF.Sqrt,
                                                            bias=epsb[:, 0:1], scale=1.0), SR + [EPSB.reg], SR)
                    k.op("dve", lambda: nc.vector.reciprocal(out=small[:, 15:16], in_=small[:, 14:15]), SR, SR)
                    k.op("dve", lambda: nc.vector.tensor_scalar(
                        out=small[:, 16:17], in0=small[:, 12:13], scalar1=small[:, 15:16], scalar2=-1.0,
                        op0=ALU.mult, op1=ALU.mult), SR, SR)
                    k.op("act", lambda: nc.scalar.activation(out=vg, in_=vg, func=AF.Identity,
                                                            scale=small[:, 15:16], bias=small[:, 16:17]),
                         [VG.reg] + SR, [VG.reg])
                    k.op("dve", lambda: nc.vector.tensor_tensor(out=vg, in0=vg, in1=sgg[:, l, :], op=ALU.mult),
                         [VG.reg, SGG.reg], [VG.reg])
                    k.op("dve", lambda tb=tb: nc.vector.tensor_tensor(out=vln[:, tb, :], in0=vg, in1=sgbb[:, l, :],
                                                                     op=ALU.add),
                         [VG.reg, SGBB.reg], [VLN.sub(tb * 1536, 1536)])
                kinds = [("a", g) for g in range(4)]
                for c in range(6):
                    kinds += [("gate", c), ("val", c)]
                kinds += [("zu", c) for c in range(6)]
                wbuf = None
                for ci, (kind, idx) in enumerate(kinds):
                    if ci % 4 == 0:
                        wbuf = w_get(l, 2 + ci // 4)
                    wv = wbuf.v(BF16)
                    bank, bankr = nextbank()
                    for kc in range(KC):
                        e0 = ((ci % 4) * 16 + kc) * 128
                        k.op("pe", lambda kc=kc, e0=e0, wv=wv, bank=bank: nc.tensor.matmul(
                            bank, lhsT=wv[:, e0:e0 + 128], rhs=hT[:, kc, :], start=(kc == 0), stop=(kc == KC - 1)),
                            [wbuf.reg, hr(kc)], [bankr], inc=(kc == KC - 1))
                    if kind == "gate":
                        fr = FT.sub((idx % 2) * 2048, 2048)
                        k.op("act", lambda idx=idx, bank=bank: nc.scalar.activation(
                            out=ft[:, idx % 2, :], in_=bank, func=AF.Sigmoid), [bankr], [fr])
                    elif kind == "val":
                        fr = FT.sub((idx % 2) * 2048, 2048)
                        k.op("dve", lambda idx=idx, bank=bank: nc.vector.tensor_tensor(
                            out=hcT[:, idx, 32:544], in0=bank, in1=ft[:, idx % 2, :], op=ALU.mult),
                            [bankr, fr], [hcr(idx)])
                    elif kind == "a":
                        k.op("act", lambda idx=idx, bank=bank: nc.scalar.activation(
                            out=aT[:, idx, 16:528], in_=bank, func=AF.Copy), [bankr], [ar(idx)])
                    else:
                        k.op("act", lambda idx=idx, bank=bank: nc.scalar.activation(
                            out=uT[:, idx, :], in_=bank, func=AF.Gelu), [bankr], [ur(idx)])
                for g in range(4):
                    w = 2 << g
                    src, srcr = aT[:, g, :], ar(g)
                    c = 1
                    i = 0
                    while c < w:
                        dst, dstr = ptmp[:, i % 2, :], PT.sub((i % 2) * 2112, 2112)
                        k.op("dve", lambda src=src, dst=dst, c=c: nc.vector.tensor_tensor(
                            out=dst[:, 2 * c - 1:528], in0=src[:, 2 * c - 1:528], in1=src[:, c - 1:528 - c],
                            op=ALU.add), [srcr], [dstr])
                        src, srcr = dst, dstr
                        c *= 2
                        i += 1
                    k.op("dve", lambda src=src, g=g, w=w: nc.vector.scalar_tensor_tensor(
                        out=cvT[:, g, :], in0=src[:, 16:528], scalar=1.0 / w, in1=aT[:, g, 16:528],
                        op0=ALU.mult, op1=ALU.subtract), [srcr, ar(g)], [cvr(g)])
                    if seq_start:
                        k.op("dve", lambda src=src, g=g: nc.vector.tensor_tensor(
                            out=small[:, 32:48], in0=src[:, 16:32], in1=INVC.v(F32)[:, g * 16:(g + 1) * 16],
                            op=ALU.mult), [srcr, INVC.reg], [SMALL.reg])
                        k.op("dve", lambda g=g: nc.vector.tensor_tensor(
                            out=cvT[:, g, 0:16], in0=small[:, 32:48], in1=aT[:, g, 16:32], op=ALU.subtract),
                            [SMALL.reg, ar(g)], [cvr(g)])
                    bank, bankr = nextbank()
                    k.op("pe", lambda g=g, bank=bank: nc.tensor.matmul(
                        bank, lhsT=wpb[:, l * 4 + g, :], rhs=cvT[:, g, :], start=True, stop=True),
                        [WPB.reg, cvr(g)], [bankr])
                    k.op("act", lambda g=g, bank=bank: nc.scalar.activation(
                        out=yT[:, g, :], in_=bank, func=AF.Identity, scale=ppv("ls", l, g), bias=0.0),
                        [bankr, PP.reg], [yr(g)])
                k.op("pool", lambda: nc.gpsimd.tensor_copy(out=ahalo[:, l], in_=aT[:, :, 512:528]),
                     [AT.reg], [AHALO.reg])
                cpend = []
                for c in range(6):
                    diag, dgr = diags[c % 2]
                    k.op("pool", lambda c=c, diag=diag: nc.gpsimd.tensor_tensor(
                        out=diag, in0=idb.unsqueeze(1).to_broadcast([128, 31, 128]),
                        in1=ppv("convw", l, c * 31, 31).unsqueeze(2).to_broadcast([128, 31, 128]),
                        op=ALU.mult), [IDB.reg, PP.reg], [dgr])
                    bank, bankr = nextbank()
                    for kk in range(31):
                        k.op("pe", lambda c=c, kk=kk, bank=bank, diag=diag: nc.tensor.matmul(
                            bank, lhsT=diag[:, kk, :], rhs=hcT[:, c, 32 - kk:544 - kk], start=(kk == 0),
                            stop=(kk == 30)), [dgr, hcr(c)], [bankr], inc=(kk == 30))
                    sr_ = SQT.sub((c % 2) * 1024, 1024)
                    k.op("act", lambda c=c, bank=bank: nc.scalar.activation(
                        out=cvT[:, c, :], in_=bank, func=AF.Identity, bias=ppv("convb", l, c), scale=1.0),
                        [bankr, PP.reg], [cvr(c)])
                    k.op("act", lambda c=c, bank=bank: nc.scalar.activation(
                        out=sqt[:, c % 2, :], in_=bank, func=AF.Square, bias=ppv("convb", l, c), scale=1.0),
                        [bankr, PP.reg], [sr_])

                    def cstats(c=c, sr_=sr_):
                        k.op("pe", lambda: nc.tensor.matmul(banks[SUMB], lhsT=oneb, rhs=cvT[:, c, :],
                                                           start=(c == 0), stop=(c == 5)),
                             [ONEB.reg, cvr(c)], [bank_r[SUMB]], inc=(c == 5))
                        k.op("pe", lambda: nc.tensor.matmul(banks[SQB], lhsT=oneb, rhs=sqt[:, c % 2, :],
                                                           start=(c == 0), stop=(c == 5)),
                             [ONEB.reg, sr_], [bank_r[SQB]], inc=(c == 5))
                    cpend.append(cstats)
                    if len(cpend) > 1:
                        cpend.pop(0)()
                for p_ in cpend:
                    p_()
                k.op("pool", lambda: nc.gpsimd.tensor_copy(out=hhalo[:, l], in_=hcT[:, :, 512:544]),
                     [HC.reg], [HHALO.reg])
                for h in range(6):
                    bank, bankr = nextbank()
                    for tb in range(4):
                        k.op("pe", lambda h=h, tb=tb, bank=bank: nc.tensor.matmul(
                            bank[:, tb * 128:(tb + 1) * 128], lhsT=vln[:, tb, h * 128:(h + 1) * 128],
                            rhs=wmt[:, l * 6 + h, :], start=True, stop=False),
                            [VLN.sub(tb * 1536, 1536), WMT.reg], [bankr], inc=False)
                        k.op("pe", lambda h=h, tb=tb, bank=bank: nc.tensor.matmul(
                            bank[:, tb * 128:(tb + 1) * 128], lhsT=oner[0:1, 0:128],
                            rhs=bsr[0:1, (l * 6 + h) * 128:(l * 6 + h + 1) * 128], start=False, stop=True),
                            [ONER.reg, BSR.reg], [bankr], inc=(tb == 3))
                    k.op("dve", lambda h=h, bank=bank: nc.vector.tensor_tensor(
                        out=yT[:, 10 + h, :], in0=bank, in1=uT[:, h, :], op=ALU.mult),
                        [bankr, ur(h)], [yr(10 + h)])
                ln_stats(768.0, 0)
                for c in range(6):
                    fr = FT.sub((c % 2) * 2048, 2048)
                    k.op("dve", lambda c=c: nc.vector.scalar_tensor_tensor(
                        out=ft[:, c % 2, :], in0=banks[SUMB], scalar=-1.0 / 768.0, in1=cvT[:, c, :],
                        op0=ALU.mult, op1=ALU.add), [cvr(c), bank_r[SUMB]], [fr])
                    k.op("dve", lambda c=c: nc.vector.tensor_tensor(out=ft[:, c % 2, :], in0=banks[SQB],
                                                                   in1=ft[:, c % 2, :], op=ALU.mult),
                         [fr, bank_r[SQB]], [fr])
                    k.op("act", lambda c=c: nc.scalar.activation(
                        out=yT[:, 4 + c, :], in_=ft[:, c % 2, :], func=AF.Silu,
                        scale=ppv("lncg", l, c), bias=ppv("lncb", l, c)), [fr, PP.reg], [yr(4 + c)])
                if stop_after == 'mixer':
                    return nc
                pend = []
                for oc in range(KC):
                    if oc % 4 == 0:
                        wbuf = w_get(l, 8 + oc // 4)
                    wv = wbuf.v(BF16)
                    bank, bankr = nextbank()
                    korder = [0, 1, 2, 3, 10, 11, 12, 13, 14, 15, 4, 5, 6, 7, 8, 9]
                    for ki, kc in enumerate(korder):
                        e0 = ((oc % 4) * 16 + kc) * 128
                        k.op("pe", lambda kc=kc, ki=ki, e0=e0, wv=wv, bank=bank: nc.tensor.matmul(
                            bank, lhsT=wv[:, e0:e0 + 128], rhs=yT[:, kc, :], start=(ki == 0), stop=(ki == KC - 1)),
                            [wbuf.reg, yr(kc)], [bankr], inc=(ki == KC - 1))
                    resid_evac(l, b, GMA, oc, bank, bankr, pend)
                main_ln(l, b, GMA, A1, B1, "lnmg", "lnmb", True, pend)
                for j in range(64):
                    if j % 4 == 0:
                        wbuf = w_get(l, 12 + j // 4)
                    wv = wbuf.v(BF16)
                    bank, bankr = nextbank()
                    for kc in range(KC):
                        e0 = ((j % 4) * 16 + kc) * 128
                        k.op("pe", lambda kc=kc, e0=e0, wv=wv, bank=bank: nc.tensor.matmul(
                            bank, lhsT=wv[:, e0:e0 + 128], rhs=hT[:, kc, :], start=(kc == 0), stop=(kc == KC - 1)),
                            [wbuf.reg, hr(kc)], [bankr], inc=(kc == KC - 1))
                    fr = FT.sub((j % 2) * 2048, 2048)
                    k.op("act", lambda j=j, bank=bank: nc.scalar.activation(out=ft[:, j % 2, :], in_=bank,
                                                                           func=AF.Square), [bankr], [fr])
                    k.op("dve", lambda j=j, bank=bank: nc.vector.scalar_tensor_tensor(
                        out=h1T[:, j, :], in0=bank, scalar=0.0, in1=ft[:, j % 2, :], op0=ALU.is_gt, op1=ALU.mult),
                        [bankr, fr], [h1r(j)])
                pend = []
                for oc in range(KC):
                    wbuf = w_get(l, 28 + oc)
                    wv = wbuf.v(BF16)
                    bank, bankr = nextbank()
                    for j in range(64):
                        k.op("pe", lambda j=j, wv=wv, bank=bank: nc.tensor.matmul(
                            bank, lhsT=wv[:, j * 128:(j + 1) * 128], rhs=h1T[:, j, :], start=(j == 0),
                            stop=(j == 63)), [wbuf.reg, h1r(j)], [bankr], inc=(j == 63))
                    resid_evac(l, b, GFA, oc, bank, bankr, pend)
                main_ln(l, b, GFA, A2, B2, "lnfg", "lnfb", not last, pend)
            for tb in range(4):
                st = STG[tb % 2]
                sv = st.v(F32)
                for q in range(4):
                    bank, bankr = nextbank()
                    for j in range(4):
                        kc = q * 4 + j
                        k.op("pe", lambda kc=kc, j=j, bank=bank, tb=tb: nc.tensor.transpose(
                            bank[:, j * 128:(j + 1) * 128], xT[:, kc, tb * 128:(tb + 1) * 128], id32),
                            [xr(kc), ID32.reg], [bankr], inc=(j == 3))
                    k.op("act", lambda q=q, bank=bank, sv=sv: nc.scalar.activation(
                        out=sv[:, q * 512:(q + 1) * 512], in_=bank, func=AF.Copy), [bankr], [st.reg])
                k.dma("act", y_d[tok0 + tb * 128:tok0 + (tb + 1) * 128, :], sv, [st.reg], [R_y], osem[tb % 2])
        for s_ in osem:
            nc.scalar.wait_ge(s_.h, s_.count)
        assert wst["taken"] == len(order)
    return nc


def build_mod_program():
    nc = bass.Bass("TRN2", target_bir_lowering=False)
    nloc_gc = NGC // NCORES
    nloc_grp = nloc_gc // 4
    wmod_d = nc.dram_tensor("wmod", [nloc_grp * 128, 8192], F32, kind="ExternalInput").ap()
    bmod_d = nc.dram_tensor("bmod", [128, nloc_gc], F32, kind="ExternalInput").ap()
    cT_d = nc.dram_tensor("cT", [128, 256], F32, kind="ExternalInput").ap()
    modl_d = nc.dram_tensor("modl", [128, nloc_gc * 16], F32, kind="ExternalOutput").ap()
    Region._all = {}
    es = ExitStack()
    with es:
        k = K(nc, es)
        arena_t = es.enter_context(nc.sbuf_tensor("arena", [128, 70000], U8))
        arena = arena_t[:, :]
        psum_t = es.enter_context(nc.psum_tensor("ps", [128, 512], F32))
        modps = psum_t[:, :]
        bankr = Region("ps", 0, 2048, "bank")
        WMB = [Buf(arena, i * 32768, 32768, f"wmb{i}") for i in range(2)]
        CACT = Buf(arena, 65536, 1024, "cact")
        BMOD = Buf(arena, 66560, nloc_gc * 4, "bmod")
        MODL = Buf(arena, 66688, nloc_gc * 16 * 4, "modl")
        csem = k.newsem("csem")
        osem = k.newsem("osem")
        wmsem = [k.newsem(f"wmsem{i}") for i in range(2)]
        k.dma("sp", CACT.v(F32), cT_d, [], [CACT.reg], csem)
        k.dma("sp", BMOD.v(F32), bmod_d, [], [BMOD.reg], csem)
        CACT.reg.w = (csem, csem.count)
        k.op("act", lambda: nc.scalar.activation(out=CACT.v(F32), in_=CACT.v(F32), func=AF.Silu),
             [CACT.reg], [CACT.reg])
        cact = CACT.v(F32).rearrange("p (k b) -> p k b", b=16)
        for grp in range(nloc_grp):
            wb_ = WMB[grp % 2]
            k.dma("sp", wb_.v(F32), wmod_d[grp * 128:(grp + 1) * 128, :], [], [wb_.reg], wmsem[grp % 2])
            wv = wb_.v(F32)
            for c4 in range(4):
                gcl = grp * 4 + c4
                for kc in range(KC):
                    k.op("pe", lambda c4=c4, kc=kc, gcl=gcl, wv=wv: nc.tensor.matmul(
                        modps[:, gcl * 16:(gcl + 1) * 16],
                        lhsT=wv[:, (c4 * 16 + kc) * 128:(c4 * 16 + kc + 1) * 128],
                        rhs=cact[:, kc, :], start=(kc == 0), stop=(kc == KC - 1)),
                        [wb_.reg, CACT.reg], [bankr], inc=(kc == KC - 1))
        k.op("dve", lambda: nc.vector.tensor_tensor(
            out=MODL.v(F32).rearrange("p (c b) -> p c b", b=16),
            in0=modps[:, 0:nloc_gc * 16].rearrange("p (c b) -> p c b", b=16),
            in1=BMOD.v(F32).unsqueeze(2).to_broadcast([128, nloc_gc, 16]),
            op=ALU.add), [bankr, BMOD.reg], [MODL.reg])
        k.dma("sp", modl_d, MODL.v(F32), [MODL.reg], [], osem)
        nc.sync.wait_ge(osem.h, osem.count)
    return nc


def _chunk_a(W, col):
    return W[:, col:col + 128].reshape(16, 128, 128).transpose(1, 0, 2).reshape(128, 2048)


def _slots_for_layer(w_in, w_out, w_ff1, w_ff2):
    sl = np.zeros((NSLOT, 128, SLOTE), np.float32)
    cols = [128 * g for g in range(4)]
    for c in range(6):
        cols += [1280 + 128 * c, 512 + 128 * c]
    cols += [2048 + 128 * c for c in range(6)]
    for ci, col in enumerate(cols):
        sl[2 + ci // 4][:, (ci % 4) * 2048:(ci % 4 + 1) * 2048] = _chunk_a(w_in, col)
    for half in range(2):
        c0 = 2816 + 384 * half
        sl[half][:, :16 * 384] = w_in[:, c0:c0 + 384].reshape(16, 128, 384).transpose(1, 0, 2).reshape(128, 6144)
    for oc in range(16):
        sl[8 + oc // 4][:, (oc % 4) * 2048:(oc % 4 + 1) * 2048] = _chunk_a(w_out, oc * 128)
    for j in range(64):
        sl[12 + j // 4][:, (j % 4) * 2048:(j % 4 + 1) * 2048] = _chunk_a(w_ff1, j * 128)
    for oc in range(16):
        sl[28 + oc] = w_ff2[:, oc * 128:(oc + 1) * 128].reshape(64, 128, 128).transpose(1, 0, 2).reshape(128, 8192)
    return sl


def _prep_shared(inp):
    f = lambda a: np.ascontiguousarray(np.asarray(a, dtype=np.float32))
    slots = np.concatenate([_slots_for_layer(f(inp["w_in"][l]), f(inp["w_out"][l]), f(inp["w_ff1"][l]),
                                             f(inp["w_ff2"][l])) for l in range(L_ALL)], axis=0)
    slots = slots.reshape(L_ALL * NSLOT * 128, SLOTE)
    w_mod = f(inp["w_mod"])
    wmod = np.concatenate([w_mod[l].reshape(16, 128, 24, 512).transpose(2, 1, 0, 3).reshape(24 * 128, 8192)
                           for l in range(L_ALL)], axis=0)
    bmod = f(inp["b_mod"]).reshape(L_ALL * 96, 128).T.copy()
    cT = f(inp["c"]).T.reshape(16, 128, 16).transpose(1, 0, 2).reshape(128, 256).copy()

    def pch(a, n):
        return f(a).reshape(L_ALL, n, 128).transpose(2, 0, 1)
    convw = f(inp["conv_w"])[:, ::-1, :].reshape(L_ALL, 31, 6, 128).transpose(3, 0, 2, 1)
    parts = [convw.reshape(128, -1), pch(inp["conv_b"], 6).reshape(128, -1), pch(inp["ln_conv_g"], 6).reshape(128, -1),
             pch(inp["ln_conv_b"], 6).reshape(128, -1), pch(inp["ls_pool"], 4).reshape(128, -1),
             pch(inp["ln_mix_g"], 16).reshape(128, -1), pch(inp["ln_mix_b"], 16).reshape(128, -1),
             pch(inp["ln_ff_g"], 16).reshape(128, -1), pch(inp["ln_ff_b"], 16).reshape(128, -1)]
    pp = np.ascontiguousarray(np.concatenate(parts, axis=1))
    sgb = np.concatenate([f(inp["ln_sgu_g"]).reshape(-1), f(inp["ln_sgu_b"]).reshape(-1)])[None, :].copy()
    bsr = f(inp["b_sgu"]).reshape(1, -1).copy()
    wp = f(inp["w_pool"]).transpose(2, 0, 1, 3).reshape(128, -1).copy()
    wst = f(inp["w_sgu"]).transpose(3, 0, 1, 2).reshape(128, -1).copy()
    return dict(slots=slots, wmod=wmod, bmod=bmod, cT=cT, pp=pp, sgb=sgb, bsr=bsr, wp=wp, wst=wst)


def _mod_maps(sh):
    nloc_grp = (NGC // 4) // NCORES
    nloc_gc = NGC // NCORES
    return [{"wmod": sh["wmod"][r * nloc_grp * 128:(r + 1) * nloc_grp * 128],
             "bmod": np.ascontiguousarray(sh["bmod"][:, r * nloc_gc:(r + 1) * nloc_gc]),
             "cT": sh["cT"]} for r in range(NCORES)]


def _mod2_for(modl_list, b0, b1):
    full = np.concatenate([np.asarray(m, dtype=np.float32).reshape(128, NGC // NCORES, 16) for m in modl_list], axis=1)
    return np.ascontiguousarray(full[:, :, [b0, b1]].reshape(128, NGC * 2))


def _main_maps(sh, xs, batches):
    maps = []
    for r in range(len(xs)):
        c2 = np.ascontiguousarray(sh["cT"].reshape(128, 16, 16)[:, :, list(batches[r])].reshape(128, 32))
        maps.append({"x": xs[r], "wsl": sh["slots"], "wmod": sh["wmod"], "bmod": sh["bmod"], "c2": c2,
                     "pp": sh["pp"], "sgb": sh["sgb"], "bsr": sh["bsr"], "wp": sh["wp"], "wst": sh["wst"]})
    return maps


def kernel(**inputs):
    x = np.asarray(inputs["x"], dtype=np.float32)
    B, S_, D_ = x.shape
    sh = _prep_shared(inputs)
    xs = [np.ascontiguousarray(x[2 * r:2 * r + 2].reshape(2 * S_, D_)) for r in range(NCORES)]
    nc = build_program((2 * S_) // T, list(range(L_ALL)))
    res = run_bass_kernel_spmd(nc, _main_maps(sh, xs, [(2 * r, 2 * r + 1) for r in range(NCORES)]),
                               core_ids=list(range(NCORES)))
    out = np.stack([np.asarray(res.results[r]["y"], dtype=np.float32).reshape(2, S_, D_) for r in range(NCORES)])
    return out.reshape(B, S_, D_)
```

```python
import numpy as np
import concourse.bass as bass
import concourse.mybir as mybir
from concourse.bass_utils import run_bass_kernel_spmd
from contextlib import ExitStack

F32 = mybir.dt.float32
BF16 = mybir.dt.bfloat16
U8 = mybir.dt.uint8
I32 = mybir.dt.int32
F32R = mybir.dt.float32r
AF = mybir.ActivationFunctionType
ALU = mybir.AluOpType

D = 2048
KC = 16
T = 512
L_ALL = 2
SEQ = 2048
NSLOT = 44
SLOTE = 8192
ALPHA = (2.0 * L_ALL) ** 0.25
EPS = 1e-5
EPSM = EPS / (ALPHA * ALPHA)
NCORES = 8
NGC = 192


class Sem:
    def __init__(self, h):
        self.h = h
        self.count = 0


class Region:
    _all = {}

    def __init__(self, space, lo, hi, name=""):
        self.space, self.lo, self.hi, self.name = space, lo, hi, name
        self.w = None
        self.r = {}
        self._ov = None
        self._ver = -1
        Region._all.setdefault(space, []).append(self)

    def overlaps(self):
        lst = Region._all[self.space]
        if self._ver != len(lst):
            self._ov = [o for o in lst if o.lo < self.hi and self.lo < o.hi]
            self._ver = len(lst)
        return self._ov


class Eng:
    def __init__(self, h, sem, is_pe=False):
        self.h, self.sem, self.is_pe = h, sem, is_pe
        self.known = {}


class K:
    def __init__(self, nc, es):
        self.nc, self.es = nc, es
        self.eng = {}
        for name, h in (("pe", nc.tensor), ("act", nc.scalar), ("dve", nc.vector),
                        ("pool", nc.gpsimd), ("sp", nc.sync)):
            self.eng[name] = Eng(h, self.newsem("e_" + name), is_pe=(name == "pe"))

    def newsem(self, name):
        return Sem(self.es.enter_context(self.nc.semaphore(name)))

    def _waits(self, E, reads, writes):
        deps = {}

        def add(tk):
            s, v = tk
            if deps.get(s, 0) < v:
                deps[s] = v
        for R in reads:
            for O in R.overlaps():
                if O.w is not None:
                    add(O.w)
        for R in writes:
            for O in R.overlaps():
                if O.w is not None:
                    add(O.w)
                for tk in O.r.items():
                    add(tk)
        for s, v in deps.items():
            if s is E.sem and E.is_pe:
                continue
            if E.known.get(s, 0) >= v:
                continue
            E.h.wait_ge(s.h, v)
            E.known[s] = v

    def op(self, eng, fn, reads=(), writes=(), inc=True):
        E = self.eng[eng]
        self._waits(E, reads, writes)
        ins = fn()
        v = E.sem.count + 1
        if inc:
            ins.then_inc(E.sem.h, 1)
            E.sem.count += 1
        for R in reads:
            if R.r.get(E.sem, 0) < v:
                R.r[E.sem] = v
        for R in writes:
            R.w = (E.sem, v)
            R.r = {}
        return ins

    def dma(self, q, out, in_, reads, writes, sem, **kw):
        E = self.eng[q]
        self._waits(E, reads, writes)
        ins = E.h.dma_start(out=out, in_=in_, **kw)
        ins.then_inc(sem.h, 16)
        sem.count += 16
        for R in reads:
            R.r[sem] = sem.count
        for R in writes:
            R.w = (sem, sem.count)
            R.r = {}
        return ins


class Buf:
    def __init__(self, arena_ap, lo, nbytes, name):
        self.a, self.lo, self.nbytes, self.name = arena_ap, lo, nbytes, name
        self.reg = Region("sb", lo, lo + nbytes, name)
        self._subs = {}

    def v(self, dt, lo=0, n=None):
        esz = 4 if dt in (F32, I32, F32R) else (2 if dt == BF16 else 1)
        if n is None:
            n = (self.nbytes - lo) // esz
        return self.a[:, self.lo + lo:self.lo + lo + n * esz].bitcast(dt)

    def sub(self, lo, nbytes):
        key = (lo, nbytes)
        if key not in self._subs:
            self._subs[key] = Region("sb", self.lo + lo, self.lo + lo + nbytes, f"{self.name}[{lo}]")
        return self._subs[key]


def build_program(nt, layers, stop_after=None):
    n_cores, gather = 1, False
    nc = bass.Bass("TRN2", target_bir_lowering=False)
    ntok = nt * T
    nloc_slots = (L_ALL * NSLOT) // n_cores
    nloc_gc = NGC // n_cores
    nloc_grp = nloc_gc // 4

    def dram(name, shape, dt, kind):
        return nc.dram_tensor(name, shape, dt, kind=kind).ap()
    x_d = dram("x", [ntok, D], F32, "ExternalInput")
    y_d = dram("y", [ntok, D], F32, "ExternalOutput")
    wsl_d = dram("wsl", [nloc_slots * 128, SLOTE], F32, "ExternalInput")
    wmod_d = dram("wmod", [NGC // 4 * 128, 8192], F32, "ExternalInput")
    bmod_d = dram("bmod", [128, NGC], F32, "ExternalInput")
    c2_d = dram("c2", [128, 32], F32, "ExternalInput")
    NPP = L_ALL * (6 * 31 + 6 * 3 + 4 + 16 * 4)
    pp_d = dram("pp", [128, NPP], F32, "ExternalInput")
    sgb_d = dram("sgb", [1, L_ALL * 2 * 768], F32, "ExternalInput")
    bsr_d = dram("bsr", [1, L_ALL * 6 * 128], F32, "ExternalInput")
    wp_d = dram("wp", [128, L_ALL * 4 * 128], F32, "ExternalInput")
    wst_d = dram("wst", [128, L_ALL * 6 * 128], F32, "ExternalInput")
    wbfull_d = dram("wbfull", [L_ALL * NSLOT * 128, SLOTE], BF16, "Internal")
    wbsh_d = wbfull_d

    Region._all = {}
    es = ExitStack()
    with es:
        k = K(nc, es)
        ARENA = 212800
        arena_t = es.enter_context(nc.sbuf_tensor("arena", [128, ARENA], U8))
        arena = arena_t[:, :]
        psum_t = es.enter_context(nc.psum_tensor("ps", [128, 8 * 512], F32))
        psum = psum_t[:, :]
        banks = [psum[:, i * 512:(i + 1) * 512] for i in range(8)]
        bank_r = [Region("ps", i * 2048, (i + 1) * 2048, f"bank{i}") for i in range(8)]
        rr = [0]

        def nextbank():
            i = rr[0] % 6
            rr[0] += 1
            return banks[i], bank_r[i]
        SUMB, SQB = 6, 7

        cur = [0]

        def alloc(name, nbytes, at=None):
            if at is None:
                lo = (cur[0] + 63) // 64 * 64
                cur[0] = lo + nbytes
                assert cur[0] <= ARENA, (name, cur[0])
            else:
                lo = at
            return Buf(arena, lo, nbytes, name)

        XT = alloc("xT", 32768)
        HT = alloc("hT", 16384)
        U0 = (cur[0] + 63) // 64 * 64
        cur[0] = U0 + 65536
        WR = [alloc(f"wr{i}", 16384) for i in range(3)]
        uo = [U0]

        def ualloc(name, nbytes):
            lo = (uo[0] + 63) // 64 * 64
            uo[0] = lo + nbytes
            assert uo[0] <= U0 + 65536, (name, uo[0] - U0)
            return Buf(arena, lo, nbytes, name)
        YT = ualloc("yT", 16384)
        AT = ualloc("aT", 4 * 528 * 4)
        HC = ualloc("hcT", 6 * 544 * 2)
        CV = ualloc("cvT", 6 * 512 * 2)
        UT = ualloc("uT", 6 * 512 * 2)
        VLN = ualloc("vln", 4 * 768 * 2)
        VG = ualloc("vg", 768 * 4)
        PT = ualloc("ptmp", 2 * 528 * 4)
        DG = ualloc("diag", 31 * 128 * 2)
        H1 = Buf(arena, U0, 65536, "h1T")
        DG2 = Buf(arena, AT.lo, 31 * 128 * 2, "diag2")
        STG = [Buf(arena, U0 + i * 8192, 8192, f"stg{i}") for i in range(2)]
        WMB = [Buf(arena, U0 + i * 16384, 16384, f"wmb{i}") for i in range(4)]
        MF = Buf(arena, XT.lo, NGC * 16 * 4, "mf")
        WP32 = Buf(arena, XT.lo + 12288, L_ALL * 4 * 128 * 4, "wp32")
        WST32 = Buf(arena, XT.lo + 16384, L_ALL * 6 * 128 * 4, "wst32")
        ONESF = Buf(arena, XT.lo + 22528, 512, "onesf")
        IOTA = Buf(arena, XT.lo + 23040, 64, "iota")
        IOTF = Buf(arena, XT.lo + 23104, 64, "iotf")
        MSEL = Buf(arena, XT.lo + 23168, NGC * 16 * 4, "msel")
        RB = alloc("rb", 3 * 1024)
        RSQ = alloc("rsq", 3 * 1024)
        MEAN = alloc("mean", 2048)
        RSTD = alloc("rstd", 2048)
        TMPA = alloc("tmpa", 2048)
        FT = alloc("ft", 2 * 2048)
        SQT = alloc("sqt", 2 * 1024)
        ID32 = alloc("id32", 512)
        IDB = alloc("idb", 256)
        ONEB = alloc("oneb", 256)
        ONER = alloc("oner", 512)
        WMT = alloc("wmt", L_ALL * 6 * 128 * 2)
        WPB = alloc("wpb", L_ALL * 4 * 128 * 2)
        SGG = alloc("sgg", L_ALL * 768 * 4)
        SGBB = alloc("sgbb", L_ALL * 768 * 2)
        SGB32 = Buf(arena, XT.lo + 40960, L_ALL * 768 * 4, "sgb32")
        BSR = alloc("bsr", L_ALL * 6 * 128 * 4)
        PP = alloc("pp", NPP * 4)
        MOD2 = alloc("mod2", NGC * 2 * 4)
        DER = alloc("der", 9 * L_ALL * 32 * 4)
        INVC = alloc("invc", 4 * 16 * 4)
        AHALO = alloc("ahalo", L_ALL * 4 * 16 * 4)
        HHALO = alloc("hhalo", L_ALL * 6 * 32 * 2)
        SMALL = alloc("small", 256)
        CACT = Buf(arena, XT.lo + 36864, 256 * 4, "cact")
        SEL = Buf(arena, XT.lo + 39552, 32 * 4, "sel")
        EPSB = alloc("eps", 16)
        if nloc_gc <= 32:
            MODL = Buf(arena, XT.lo + 37888, nloc_gc * 16 * 4, "modl")
            BMOD = Buf(arena, XT.lo + 39424, nloc_gc * 4, "bmod")
        else:
            MODL = Buf(arena, WR[0].lo, NGC * 16 * 4, "modl")
            BMOD = Buf(arena, WR[1].lo, NGC * 4, "bmod")

        ppo = {}
        o = 0
        for nm, n in (("convw", 6 * 31), ("convb", 6), ("lncg", 6), ("lncb", 6), ("ls", 4),
                      ("lnmg", 16), ("lnmb", 16), ("lnfg", 16), ("lnfb", 16)):
            ppo[nm] = (o, n)
            o += L_ALL * n
        assert o == NPP

        def ppv(nm, l, i0=0, n=1):
            o, per = ppo[nm]
            a = o + l * per + i0
            return PP.v(F32)[:, a:a + n]

        csem = k.newsem("csem")
        castsem = k.newsem("castsem")
        gsem = k.newsem("gsem")
        msem = k.newsem("msem")
        wmsem = [k.newsem(f"wmsem{i}") for i in range(4)]
        ringsem = [k.newsem(f"ring{i}") for i in range(3)]
        stgsem = [k.newsem(f"stgs{i}") for i in range(2)]
        osem = [k.newsem(f"osem{i}") for i in range(2)]

        R_wbsh = Region("d_wbsh", 0, 1, "wbsh")
        R_wbfull = Region("d_wbfull", 0, 1, "wbfull") if gather else R_wbsh
        R_modsh = Region("d_modsh", 0, 1, "modsh")
        R_modfull = Region("d_modfull", 0, 1, "modfull") if gather else R_modsh
        R_y = Region("d_y", 0, 1, "y")

        C2 = Buf(arena, XT.lo + 36864, 32 * 4, "c2")
        BMD = Buf(arena, XT.lo + 37888, NGC * 4, "bmd")
        for buf, src in ((PP, pp_d), (C2, c2_d), (BMD, bmod_d), (WP32, wp_d), (WST32, wst_d)):
            k.dma("sp", buf.v(F32), src, [], [buf.reg], csem)
        k.dma("sp", BSR.v(F32), bsr_d.partition_broadcast(128), [], [BSR.reg], csem)
        k.dma("sp", SGG.v(F32), sgb_d[:, 0:L_ALL * 768].partition_broadcast(128), [], [SGG.reg], csem)
        k.dma("sp", SGB32.v(F32), sgb_d[:, L_ALL * 768:2 * L_ALL * 768].partition_broadcast(128), [], [SGB32.reg], csem)
        for b_ in (PP, C2, BMD, WP32, WST32, BSR, SGG, SGB32):
            b_.reg.w = (csem, csem.count)

        R_slot = {g: Region("d_wb", g, g + 1, f"wb{g}") for g in range(L_ALL * NSLOT)}
        stsem = [k.newsem(f"wst{i}") for i in range(3)]
        ringsem_sw = [k.newsem(f"ringsw{i}") for i in range(3)]

        k.op("dve", lambda: nc.vector.memset(EPSB.v(F32)[:, 0:1], EPS), [], [EPSB.reg])
        k.op("dve", lambda: nc.vector.memset(EPSB.v(F32)[:, 1:2], EPSM), [EPSB.reg], [EPSB.reg])
        k.op("pool", lambda: nc.gpsimd.memset(ONESF.v(F32), 1.0), [], [ONESF.reg])
        k.op("pool", lambda: nc.gpsimd.memset(ONEB.v(BF16), 1.0), [], [ONEB.reg])
        k.op("pool", lambda: nc.gpsimd.memset(ONER.v(F32), 1.0), [], [ONER.reg])
        k.op("pool", lambda: nc.gpsimd.affine_select(
            out=ID32.v(F32), in_=ONESF.v(F32), pattern=[[1, 128]], compare_op=ALU.is_equal,
            fill=0.0, base=0, channel_multiplier=-1), [ONESF.reg], [ID32.reg])
        k.op("pool", lambda: nc.gpsimd.tensor_copy(out=IDB.v(BF16), in_=ID32.v(F32)), [ID32.reg], [IDB.reg])
        k.op("pool", lambda: nc.gpsimd.affine_select(
            out=WMT.v(BF16).rearrange("p (a t) -> p a t", t=128),
            in_=WST32.v(F32).rearrange("p (a t) -> p a t", t=128),
            pattern=[[0, L_ALL * 6], [1, 128]], compare_op=ALU.is_ge,
            fill=0.0, base=0, channel_multiplier=-1), [WST32.reg], [WMT.reg])
        k.op("pool", lambda: nc.gpsimd.tensor_copy(out=WPB.v(BF16), in_=WP32.v(F32)), [WP32.reg], [WPB.reg])
        k.op("pool", lambda: nc.gpsimd.tensor_copy(out=SGBB.v(BF16), in_=SGB32.v(F32)), [SGB32.reg], [SGBB.reg])
        k.op("pool", lambda: nc.gpsimd.iota(IOTA.v(I32), pattern=[[1, 16]], base=1, channel_multiplier=0),
             [], [IOTA.reg])
        k.op("dve", lambda: nc.vector.tensor_copy(out=IOTF.v(F32), in_=IOTA.v(I32)), [IOTA.reg], [IOTF.reg])
        for g in range(4):
            k.op("dve", lambda g=g: nc.vector.tensor_scalar(
                out=INVC.v(F32)[:, g * 16:(g + 1) * 16], in0=IOTF.v(F32), scalar1=float(2 << g), scalar2=None,
                op0=ALU.min), [IOTF.reg, INVC.reg], [INVC.reg])
        k.op("dve", lambda: nc.vector.reciprocal(out=INVC.v(F32), in_=INVC.v(F32)), [INVC.reg], [INVC.reg])

        id32 = ID32.v(F32)
        C2B = Buf(arena, XT.lo + 36864 + 128, 64, "c2b")
        k.op("act", lambda: nc.scalar.activation(out=C2B.v(BF16), in_=C2.v(F32), func=AF.Silu), [C2.reg], [C2B.reg])
        c2v = C2B.v(BF16).rearrange("p (k b) -> p k b", b=2)
        modps = banks[SUMB]
        MROW = [Buf(arena, XT.lo + 47104 + i * 2048, 2048, f"mrow{i}") for i in range(2)]
        for grp in range(NGC // 4):
            wb_ = WMB[grp % 4]
            k.dma("pool", wb_.v(BF16), wmod_d[grp * 128:(grp + 1) * 128, :], [], [wb_.reg], wmsem[grp % 4],
                  max_dma_last_dim=8192)
            wv = wb_.v(BF16)
            bank, bankr = nextbank()
            for kc in range(KC):
                k.op("pe", lambda kc=kc, wv=wv, bank=bank: nc.tensor.matmul(
                    bank[0:2, :], lhsT=c2v[:, kc, :], rhs=wv[:, kc * 512:(kc + 1) * 512],
                    start=(kc == 0), stop=(kc == KC - 1)), [wb_.reg, C2B.reg], [bankr], inc=(kc == KC - 1))
            mr = MROW[grp % 2]
            k.op("act", lambda bank=bank, mr=mr: nc.scalar.activation(out=mr.v(F32)[0:2, :], in_=bank[0:2, :],
                                                                     func=AF.Copy), [bankr], [mr.reg])
            for c4 in range(4):
                gc = grp * 4 + c4
                k.op("pe", lambda c4=c4, gc=gc, mr=mr: nc.tensor.matmul(
                    modps[:, gc * 2:(gc + 1) * 2], lhsT=mr.v(F32)[0:2, c4 * 128:(c4 + 1) * 128],
                    rhs=id32[0:2, 0:2], start=True, stop=True), [mr.reg, ID32.reg], [bank_r[SUMB]], inc=(c4 == 3))
        k.op("dve", lambda: nc.vector.tensor_tensor(
            out=MOD2.v(F32).rearrange("p (c b) -> p c b", b=2),
            in0=modps[:, 0:NGC * 2].rearrange("p (c b) -> p c b", b=2),
            in1=BMD.v(F32).unsqueeze(2).to_broadcast([128, NGC, 2]),
            op=ALU.add), [bank_r[SUMB], BMD.reg], [MOD2.reg])
        mod2 = MOD2.v(F32)

        def modv(l, m, kc, b):
            gc = l * 96 + m * 16 + kc
            return mod2[:, gc * 2 + b:gc * 2 + b + 1]

        def mod_all(l, m):
            gc = l * 96 + m * 16
            return mod2[:, gc * 2:(gc + 16) * 2].rearrange("p (k b) -> p k b", b=2)
        der = DER.v(F32)

        def dv(idx, l):
            a = (idx * L_ALL + l) * 32
            return der[:, a:a + 32].rearrange("p (k b) -> p k b", b=2)

        def dsc(idx, l, kc, b):
            a = (idx * L_ALL + l) * 32 + kc * 2 + b
            return der[:, a:a + 1]
        S1M, GMA, GFA, S1F, A1, B1, A2, B2, SHM = range(9)
        DR = [DER.reg, MOD2.reg, PP.reg]

        def ppb(nm, l):
            o, per = ppo[nm]
            return PP.v(F32)[:, o + l * per:o + l * per + 16].unsqueeze(2).to_broadcast([128, 16, 2])
        for l in range(L_ALL):
            k.op("dve", lambda l=l: nc.vector.tensor_scalar(out=dv(S1M, l), in0=mod_all(l, 1), scalar1=1.0,
                                                           scalar2=None, op0=ALU.add), DR, [DER.reg])
            k.op("dve", lambda l=l: nc.vector.tensor_scalar(out=dv(S1F, l), in0=mod_all(l, 4), scalar1=1.0,
                                                           scalar2=None, op0=ALU.add), DR, [DER.reg])
            k.op("dve", lambda l=l: nc.vector.tensor_scalar(out=dv(GMA, l), in0=mod_all(l, 2), scalar1=1.0 / ALPHA,
                                                           scalar2=None, op0=ALU.mult), DR, [DER.reg])
            k.op("dve", lambda l=l: nc.vector.tensor_scalar(out=dv(GFA, l), in0=mod_all(l, 5), scalar1=1.0 / ALPHA,
                                                           scalar2=None, op0=ALU.mult), DR, [DER.reg])
            k.op("dve", lambda l=l: nc.vector.tensor_copy(out=dv(SHM, l), in_=mod_all(l, 0)), DR, [DER.reg])
        for l in range(L_ALL):
            k.op("dve", lambda l=l: nc.vector.tensor_tensor(out=dv(A1, l), in0=dv(S1F, l), in1=ppb("lnmg", l),
                                                           op=ALU.mult), DR, [DER.reg])
            k.op("dve", lambda l=l: nc.vector.tensor_tensor(out=dv(B1, l), in0=dv(S1F, l), in1=ppb("lnmb", l),
                                                           op=ALU.mult), DR, [DER.reg])
            k.op("dve", lambda l=l: nc.vector.tensor_tensor(out=dv(B1, l), in0=dv(B1, l), in1=mod_all(l, 3),
                                                           op=ALU.add), DR, [DER.reg])
            if l + 1 < L_ALL:
                k.op("dve", lambda l=l: nc.vector.tensor_tensor(out=dv(A2, l), in0=dv(S1M, l + 1),
                                                               in1=ppb("lnfg", l), op=ALU.mult), DR, [DER.reg])
                k.op("dve", lambda l=l: nc.vector.tensor_tensor(out=dv(B2, l), in0=dv(S1M, l + 1),
                                                               in1=ppb("lnfb", l), op=ALU.mult), DR, [DER.reg])
                k.op("dve", lambda l=l: nc.vector.tensor_tensor(out=dv(B2, l), in0=dv(B2, l), in1=mod_all(l + 1, 0),
                                                               op=ALU.add), DR, [DER.reg])

        order = [(l, s) for _ in range(nt) for l in layers for s in range(NSLOT)]
        wst = {"issued": 0, "taken": 0}

        def w_issue():
            i = wst["issued"]
            if i >= len(order):
                return
            l, s = order[i]
            g = l * NSLOT + s
            rb_ = WR[i % 3]
            rows = slice(g * 128, (g + 1) * 128)
            if i < NSLOT * len(layers):
                k.dma("pool", rb_.v(BF16), wsl_d[rows, :], [], [rb_.reg], ringsem_sw[i % 3], max_dma_last_dim=8192)
                if nt > 1:
                    k.dma("sp", wbfull_d[rows, :], rb_.v(BF16), [rb_.reg], [R_slot[g]], stsem[i % 3])
            else:
                k.dma("sp", rb_.v(BF16), wbfull_d[rows, :], [R_slot[g]], [rb_.reg], ringsem[i % 3])
            wst["issued"] += 1

        def w_get(l, s, hold=0):
            i = wst["taken"]
            assert order[i] == (l, s), (order[i], l, s)
            while wst["issued"] <= min(i + 2 - hold, len(order) - 1):
                w_issue()
            wst["taken"] += 1
            return WR[i % 3]

        xT = XT.v(F32).rearrange("p (k t) -> p k t", t=T)
        hT = HT.v(BF16).rearrange("p (k t) -> p k t", t=T)
        yT = YT.v(BF16).rearrange("p (k t) -> p k t", t=T)
        h1T = H1.v(BF16).rearrange("p (k t) -> p k t", t=T)
        aT = AT.v(F32).rearrange("p (g t) -> p g t", t=528)
        hcT = HC.v(BF16).rearrange("p (c t) -> p c t", t=544)
        cvT = CV.v(BF16).rearrange("p (c t) -> p c t", t=T)
        uT = UT.v(BF16).rearrange("p (c t) -> p c t", t=T)
        vln = VLN.v(BF16).rearrange("p (b f) -> p b f", f=768)
        vg = VG.v(F32)
        ptmp = PT.v(F32).rearrange("p (i t) -> p i t", t=528)
        diags = [(b_.v(BF16).rearrange("p (k m) -> p k m", m=128), b_.reg) for b_ in (DG, DG2)]
        rb = RB.v(BF16).rearrange("p (i t) -> p i t", t=T)
        rsq = RSQ.v(BF16).rearrange("p (i t) -> p i t", t=T)
        ft = FT.v(F32).rearrange("p (i t) -> p i t", t=T)
        sqt = SQT.v(BF16).rearrange("p (i t) -> p i t", t=T)
        mean, rstd, tmpa = MEAN.v(F32), RSTD.v(F32), TMPA.v(F32)
        id32, idb, oneb = ID32.v(F32), IDB.v(BF16), ONEB.v(BF16)
        wmt = WMT.v(BF16).rearrange("p (a t) -> p a t", t=128)
        wpb = WPB.v(BF16).rearrange("p (a t) -> p a t", t=128)
        sgg = SGG.v(F32).rearrange("p (l f) -> p l f", f=768)
        sgbb = SGBB.v(BF16).rearrange("p (l f) -> p l f", f=768)
        bsr = BSR.v(F32)
        oner = ONER.v(F32)
        ahalo = AHALO.v(F32).rearrange("p (l g t) -> p l g t", g=4, t=16)
        hhalo = HHALO.v(BF16).rearrange("p (l c t) -> p l c t", c=6, t=32)
        small = SMALL.v(F32)
        epsb = EPSB.v(F32)

        def xr(kc):
            return XT.sub(kc * 2048, 2048)

        def hr(kc):
            return HT.sub(kc * 1024, 1024)

        def yr(kc):
            return YT.sub(kc * 1024, 1024)

        def h1r(j):
            return H1.sub(j * 1024, 1024)

        def cvr(c):
            return CV.sub(c * 1024, 1024)

        def ur(c):
            return UT.sub(c * 1024, 1024)

        def hcr(c):
            return HC.sub(c * 1088, 1088)

        def ar(g):
            return AT.sub(g * 2112, 2112)

        def ln_stats(nfeat, eps_col):
            inv = 1.0 / nfeat
            k.op("dve", lambda: nc.vector.tensor_scalar(out=mean, in0=banks[SUMB], scalar1=inv, scalar2=None,
                                                       op0=ALU.mult), [bank_r[SUMB]], [MEAN.reg])
            k.op("dve", lambda: nc.vector.tensor_tensor(out=tmpa, in0=mean, in1=mean, op=ALU.mult),
                 [MEAN.reg], [TMPA.reg])
            k.op("dve", lambda: nc.vector.scalar_tensor_tensor(out=tmpa, in0=banks[SQB], scalar=inv, in1=tmpa,
                                                              op0=ALU.mult, op1=ALU.subtract),
                 [bank_r[SQB], TMPA.reg], [TMPA.reg])
            k.op("act", lambda: nc.scalar.activation(out=rstd, in_=tmpa, func=AF.Sqrt,
                                                    bias=epsb[:, eps_col:eps_col + 1], scale=1.0),
                 [TMPA.reg, EPSB.reg], [RSTD.reg])
            k.op("dve", lambda: nc.vector.reciprocal(out=banks[SQB], in_=rstd), [RSTD.reg], [bank_r[SQB]])

        def main_ln(l, b, gidx, aidx, bidx, gname, bname, make_h, pending):
            for p_ in pending:
                p_()
            ln_stats(float(D), 1)
            for kc in range(KC):
                k.op("dve", lambda kc=kc: nc.vector.scalar_tensor_tensor(
                    out=xT[:, kc, :], in0=banks[SUMB], scalar=-1.0 / D, in1=xT[:, kc, :], op0=ALU.mult, op1=ALU.add),
                    [xr(kc), bank_r[SUMB]], [xr(kc)])
                k.op("dve", lambda kc=kc: nc.vector.tensor_tensor(out=xT[:, kc, :], in0=banks[SQB], in1=xT[:, kc, :],
                                                                 op=ALU.mult), [xr(kc), bank_r[SQB]], [xr(kc)])
                if make_h:
                    k.op("act", lambda kc=kc: nc.scalar.activation(
                        out=hT[:, kc, :], in_=xT[:, kc, :], func=AF.Identity,
                        scale=dsc(aidx, l, kc, b), bias=dsc(bidx, l, kc, b)), [xr(kc), DER.reg], [hr(kc)])
                k.op("act", lambda kc=kc: nc.scalar.activation(
                    out=xT[:, kc, :], in_=xT[:, kc, :], func=AF.Identity,
                    scale=ppv(gname, l, kc), bias=ppv(bname, l, kc)), [xr(kc), PP.reg], [xr(kc)])

        def resid_evac(l, b, gidx, oc, bank, bankr, pend):
            k.op("dve", lambda: nc.vector.scalar_tensor_tensor(
                out=xT[:, oc, :], in0=bank, scalar=dsc(gidx, l, oc, b), in1=xT[:, oc, :],
                op0=ALU.mult, op1=ALU.add), [bankr, xr(oc), DER.reg], [xr(oc)])
            i = oc % 3
            rbr, rsr = RB.sub(i * 1024, 1024), RSQ.sub(i * 1024, 1024)
            k.op("pool", lambda: nc.gpsimd.tensor_copy(out=rb[:, i, :], in_=xT[:, oc, :]), [xr(oc)], [rbr])
            k.op("pool", lambda: nc.gpsimd.tensor_tensor(out=rsq[:, i, :], in0=xT[:, oc, :], in1=xT[:, oc, :],
                                                        op=ALU.mult), [xr(oc)], [rsr])

            def stats():
                k.op("pe", lambda: nc.tensor.matmul(banks[SUMB], lhsT=oneb, rhs=rb[:, i, :], start=(oc == 0),
                                                   stop=(oc == KC - 1)), [ONEB.reg, rbr], [bank_r[SUMB]],
                     inc=(oc == KC - 1))
                k.op("pe", lambda: nc.tensor.matmul(banks[SQB], lhsT=oneb, rhs=rsq[:, i, :], start=(oc == 0),
                                                   stop=(oc == KC - 1)), [ONEB.reg, rsr], [bank_r[SQB]],
                     inc=(oc == KC - 1))
            pend.append(stats)
            if len(pend) > 2:
                pend.pop(0)()

        for ti in range(nt):
            tok0 = ti * T
            b = (tok0 // SEQ) % 2
            seq_start = (tok0 % SEQ == 0)
            for tb in range(4):
                st = STG[tb % 2]
                k.dma("sp", st.v(F32), x_d[tok0 + tb * 128:tok0 + (tb + 1) * 128, :], [], [st.reg], stgsem[tb % 2])
                sv = st.v(F32)
                for q in range(4):
                    bank, bankr = nextbank()
                    for j in range(4):
                        kc = q * 4 + j
                        k.op("pe", lambda kc=kc, j=j, bank=bank, sv=sv: nc.tensor.transpose(
                            bank[:, j * 128:(j + 1) * 128], sv[:, kc * 128:(kc + 1) * 128], id32),
                            [st.reg, ID32.reg], [bankr], inc=(j == 3))
                    k.op("dve", lambda q=q, tb=tb, bank=bank: nc.vector.tensor_copy(
                        out=xT[:, q * 4:(q + 1) * 4, tb * 128:(tb + 1) * 128],
                        in_=bank.rearrange("p (j t) -> p j t", t=128)),
                        [bankr], [xr(q * 4 + j_) for j_ in range(4)])
            for li, l in enumerate(layers):
                last = (li == len(layers) - 1)
                if li == 0:
                    for kc in range(KC):
                        k.op("act", lambda kc=kc: nc.scalar.activation(
                            out=hT[:, kc, :], in_=xT[:, kc, :], func=AF.Identity,
                            scale=dsc(S1M, l, kc, b), bias=dsc(SHM, l, kc, b)), [xr(kc), DER.reg], [hr(kc)])
                if seq_start:
                    k.op("pool", lambda: nc.gpsimd.memset(aT[:, :, 0:16], 0.0), [], [AT.reg])
                    k.op("pool", lambda: nc.gpsimd.memset(hcT[:, :, 0:32], 0.0), [], [HC.reg])
                else:
                    k.op("pool", lambda: nc.gpsimd.tensor_copy(out=aT[:, :, 0:16], in_=ahalo[:, l]),
                         [AHALO.reg], [AT.reg])
                    k.op("pool", lambda: nc.gpsimd.tensor_copy(out=hcT[:, :, 0:32], in_=hhalo[:, l]),
                         [HHALO.reg], [HC.reg])
                w6, w7 = w_get(l, 0), w_get(l, 1, hold=1)
                for tb in range(4):
                    for half, wz in enumerate((w6, w7)):
                        wv = wz.v(BF16)
                        bank, bankr = nextbank()
                        for kc in range(KC):
                            k.op("pe", lambda kc=kc, wv=wv, bank=bank, tb=tb: nc.tensor.matmul(
                                bank[:, 0:384], lhsT=hT[:, kc, tb * 128:(tb + 1) * 128],
                                rhs=wv[:, kc * 384:(kc + 1) * 384], start=(kc == 0), stop=(kc == KC - 1)),
                                [wz.reg, hr(kc)], [bankr], inc=(kc == KC - 1))
                        k.op("act", lambda half=half, bank=bank: nc.scalar.activation(
                            out=vg[:, half * 384:(half + 1) * 384], in_=bank[:, 0:384], func=AF.Gelu),
                            [bankr], [VG.reg])
                    SR = [SMALL.reg]
                    k.op("dve", lambda: nc.vector.bn_stats(out=small[:, 0:6], in_=vg[:, 0:384]), [VG.reg], SR)
                    k.op("dve", lambda: nc.vector.bn_stats(out=small[:, 6:12], in_=vg[:, 384:768]), [VG.reg] + SR, SR)
                    k.op("dve", lambda: nc.vector.bn_aggr(out=small[:, 12:14], in_=small[:, 0:12]), SR, SR)
                    k.op("act", lambda: nc.scalar.activation(out=small[:, 14:15], in_=small[:, 13:14], func=AF.Sqrt,
                                                            bias=epsb[:, 0:1], scale=1.0), SR + [EPSB.reg], SR)
                    k.op("dve", lambda: nc.vector.reciprocal(out=small[:, 15:16], in_=small[:, 14:15]), SR, SR)
                    k.op("dve", lambda: nc.vector.tensor_scalar(
                        out=small[:, 16:17], in0=small[:, 12:13], scalar1=small[:, 15:16], scalar2=-1.0,
                        op0=ALU.mult, op1=ALU.mult), SR, SR)
                    k.op("act", lambda: nc.scalar.activation(out=vg, in_=vg, func=AF.Identity,
                                                            scale=small[:, 15:16], bias=small[:, 16:17]),
                         [VG.reg] + SR, [VG.reg])
                    k.op("dve", lambda: nc.vector.tensor_tensor(out=vg, in0=vg, in1=sgg[:, l, :], op=ALU.mult),
                         [VG.reg, SGG.reg], [VG.reg])
                    k.op("dve", lambda tb=tb: nc.vector.tensor_tensor(out=vln[:, tb, :], in0=vg, in1=sgbb[:, l, :],
                                                                     op=ALU.add),
                         [VG.reg, SGBB.reg], [VLN.sub(tb * 1536, 1536)])
                kinds = [("a", g) for g in range(4)]
                for c in range(6):
                    kinds += [("gate", c), ("val", c)]
                kinds += [("zu", c) for c in range(6)]
                wbuf = None
                for ci, (kind, idx) in enumerate(kinds):
                    if ci % 4 == 0:
                        wbuf = w_get(l, 2 + ci // 4)
                    wv = wbuf.v(BF16)
                    bank, bankr = nextbank()
                    for kc in range(KC):
                        e0 = ((ci % 4) * 16 + kc) * 128
                        k.op("pe", lambda kc=kc, e0=e0, wv=wv, bank=bank: nc.tensor.matmul(
                            bank, lhsT=wv[:, e0:e0 + 128], rhs=hT[:, kc, :], start=(kc == 0), stop=(kc == KC - 1)),
                            [wbuf.reg, hr(kc)], [bankr], inc=(kc == KC - 1))
                    if kind == "gate":
                        fr = FT.sub((idx % 2) * 2048, 2048)
                        k.op("act", lambda idx=idx, bank=bank: nc.scalar.activation(
                            out=ft[:, idx % 2, :], in_=bank, func=AF.Sigmoid), [bankr], [fr])
                    elif kind == "val":
                        fr = FT.sub((idx % 2) * 2048, 2048)
                        k.op("dve", lambda idx=idx, bank=bank: nc.vector.tensor_tensor(
                            out=hcT[:, idx, 32:544], in0=bank, in1=ft[:, idx % 2, :], op=ALU.mult),
                            [bankr, fr], [hcr(idx)])
                    elif kind == "a":
                        k.op("act", lambda idx=idx, bank=bank: nc.scalar.activation(
                            out=aT[:, idx, 16:528], in_=bank, func=AF.Copy), [bankr], [ar(idx)])
                    else:
                        k.op("act", lambda idx=idx, bank=bank: nc.scalar.activation(
                            out=uT[:, idx, :], in_=bank, func=AF.Gelu), [bankr], [ur(idx)])
                for g in range(4):
                    w = 2 << g
                    src, srcr = aT[:, g, :], ar(g)
                    c = 1
                    i = 0
                    while c < w:
                        dst, dstr = ptmp[:, i % 2, :], PT.sub((i % 2) * 2112, 2112)
                        k.op("dve", lambda src=src, dst=dst, c=c: nc.vector.tensor_tensor(
                            out=dst[:, 2 * c - 1:528], in0=src[:, 2 * c - 1:528], in1=src[:, c - 1:528 - c],
                            op=ALU.add), [srcr], [dstr])
                        src, srcr = dst, dstr
                        c *= 2
                        i += 1
                    k.op("dve", lambda src=src, g=g, w=w: nc.vector.scalar_tensor_tensor(
                        out=cvT[:, g, :], in0=src[:, 16:528], scalar=1.0 / w, in1=aT[:, g, 16:528],
                        op0=ALU.mult, op1=ALU.subtract), [srcr, ar(g)], [cvr(g)])
                    if seq_start:
                        k.op("dve", lambda src=src, g=g: nc.vector.tensor_tensor(
                            out=small[:, 32:48], in0=src[:, 16:32], in1=INVC.v(F32)[:, g * 16:(g + 1) * 16],
                            op=ALU.mult), [srcr, INVC.reg], [SMALL.reg])
                        k.op("dve", lambda g=g: nc.vector.tensor_tensor(
                            out=cvT[:, g, 0:16], in0=small[:, 32:48], in1=aT[:, g, 16:32], op=ALU.subtract),
                            [SMALL.reg, ar(g)], [cvr(g)])
                    bank, bankr = nextbank()
                    k.op("pe", lambda g=g, bank=bank: nc.tensor.matmul(
                        bank, lhsT=wpb[:, l * 4 + g, :], rhs=cvT[:, g, :], start=True, stop=True),
                        [WPB.reg, cvr(g)], [bankr])
                    k.op("act", lambda g=g, bank=bank: nc.scalar.activation(
                        out=yT[:, g, :], in_=bank, func=AF.Identity, scale=ppv("ls", l, g), bias=0.0),
                        [bankr, PP.reg], [yr(g)])
                k.op("pool", lambda: nc.gpsimd.tensor_copy(out=ahalo[:, l], in_=aT[:, :, 512:528]),
                     [AT.reg], [AHALO.reg])
                cpend = []
                for c in range(6):
                    diag, dgr = diags[c % 2]
                    k.op("pool", lambda c=c, diag=diag: nc.gpsimd.tensor_tensor(
                        out=diag, in0=idb.unsqueeze(1).to_broadcast([128, 31, 128]),
                        in1=ppv("convw", l, c * 31, 31).unsqueeze(2).to_broadcast([128, 31, 128]),
                        op=ALU.mult), [IDB.reg, PP.reg], [dgr])
                    bank, bankr = nextbank()
                    for kk in range(31):
                        k.op("pe", lambda c=c, kk=kk, bank=bank, diag=diag: nc.tensor.matmul(
                            bank, lhsT=diag[:, kk, :], rhs=hcT[:, c, 32 - kk:544 - kk], start=(kk == 0),
                            stop=(kk == 30)), [dgr, hcr(c)], [bankr], inc=(kk == 30))
                    sr_ = SQT.sub((c % 2) * 1024, 1024)
                    k.op("act", lambda c=c, bank=bank: nc.scalar.activation(
                        out=cvT[:, c, :], in_=bank, func=AF.Identity, bias=ppv("convb", l, c), scale=1.0),
                        [bankr, PP.reg], [cvr(c)])
                    k.op("act", lambda c=c, bank=bank: nc.scalar.activation(
                        out=sqt[:, c % 2, :], in_=bank, func=AF.Square, bias=ppv("convb", l, c), scale=1.0),
                        [bankr, PP.reg], [sr_])

                    def cstats(c=c, sr_=sr_):
                        k.op("pe", lambda: nc.tensor.matmul(banks[SUMB], lhsT=oneb, rhs=cvT[:, c, :],
                                                           start=(c == 0), stop=(c == 5)),
                             [ONEB.reg, cvr(c)], [bank_r[SUMB]], inc=(c == 5))
                        k.op("pe", lambda: nc.tensor.matmul(banks[SQB], lhsT=oneb, rhs=sqt[:, c % 2, :],
                                                           start=(c == 0), stop=(c == 5)),
                             [ONEB.reg, sr_], [bank_r[SQB]], inc=(c == 5))
                    cpend.append(cstats)
                    if len(cpend) > 1:
                        cpend.pop(0)()
                for p_ in cpend:
                    p_()
                k.op("pool", lambda: nc.gpsimd.tensor_copy(out=hhalo[:, l], in_=hcT[:, :, 512:544]),
                     [HC.reg], [HHALO.reg])
                for h in range(6):
                    bank, bankr = nextbank()
                    for tb in range(4):
                        k.op("pe", lambda h=h, tb=tb, bank=bank: nc.tensor.matmul(
                            bank[:, tb * 128:(tb + 1) * 128], lhsT=vln[:, tb, h * 128:(h + 1) * 128],
                            rhs=wmt[:, l * 6 + h, :], start=True, stop=True),
                            [VLN.sub(tb * 1536, 1536), WMT.reg], [bankr], inc=(tb == 3))
                    fr = FT.sub((h % 2) * 2048, 2048)
                    k.op("dve", lambda h=h, bank=bank: nc.vector.tensor_tensor(
                        out=ft[:, h % 2, :].rearrange("p (b t) -> p b t", t=128),
                        in0=bank.rearrange("p (b t) -> p b t", t=128),
                        in1=bsr[:, (l * 6 + h) * 128:(l * 6 + h + 1) * 128].unsqueeze(1).to_broadcast([128, 4, 128]),
                        op=ALU.add), [bankr, BSR.reg], [fr])
                    k.op("dve", lambda h=h: nc.vector.tensor_tensor(
                        out=yT[:, 10 + h, :], in0=ft[:, h % 2, :], in1=uT[:, h, :], op=ALU.mult),
                        [fr, ur(h)], [yr(10 + h)])
                ln_stats(768.0, 0)
                for c in range(6):
                    fr = FT.sub((c % 2) * 2048, 2048)
                    k.op("dve", lambda c=c: nc.vector.scalar_tensor_tensor(
                        out=ft[:, c % 2, :], in0=banks[SUMB], scalar=-1.0 / 768.0, in1=cvT[:, c, :],
                        op0=ALU.mult, op1=ALU.add), [cvr(c), bank_r[SUMB]], [fr])
                    k.op("dve", lambda c=c: nc.vector.tensor_tensor(out=ft[:, c % 2, :], in0=banks[SQB],
                                                                   in1=ft[:, c % 2, :], op=ALU.mult),
                         [fr, bank_r[SQB]], [fr])
                    k.op("act", lambda c=c: nc.scalar.activation(
                        out=yT[:, 4 + c, :], in_=ft[:, c % 2, :], func=AF.Silu,
                        scale=ppv("lncg", l, c), bias=ppv("lncb", l, c)), [fr, PP.reg], [yr(4 + c)])
                if stop_after == 'mixer':
                    return nc
                pend = []
                for oc in range(KC):
                    if oc % 4 == 0:
                        wbuf = w_get(l, 8 + oc // 4)
                    wv = wbuf.v(BF16)
                    bank, bankr = nextbank()
                    korder = [0, 1, 2, 3, 10, 11, 12, 13, 14, 15, 4, 5, 6, 7, 8, 9]
                    for ki, kc in enumerate(korder):
                        e0 = ((oc % 4) * 16 + kc) * 128
                        k.op("pe", lambda kc=kc, ki=ki, e0=e0, wv=wv, bank=bank: nc.tensor.matmul(
                            bank, lhsT=wv[:, e0:e0 + 128], rhs=yT[:, kc, :], start=(ki == 0), stop=(ki == KC - 1)),
                            [wbuf.reg, yr(kc)], [bankr], inc=(ki == KC - 1))
                    resid_evac(l, b, GMA, oc, bank, bankr, pend)
                main_ln(l, b, GMA, A1, B1, "lnmg", "lnmb", True, pend)
                for j in range(64):
                    if j % 4 == 0:
                        wbuf = w_get(l, 12 + j // 4)
                    wv = wbuf.v(BF16)
                    bank, bankr = nextbank()
                    for kc in range(KC):
                        e0 = ((j % 4) * 16 + kc) * 128
                        k.op("pe", lambda kc=kc, e0=e0, wv=wv, bank=bank: nc.tensor.matmul(
                            bank, lhsT=wv[:, e0:e0 + 128], rhs=hT[:, kc, :], start=(kc == 0), stop=(kc == KC - 1)),
                            [wbuf.reg, hr(kc)], [bankr], inc=(kc == KC - 1))
                    fr = FT.sub((j % 2) * 2048, 2048)
                    k.op("act", lambda j=j, bank=bank: nc.scalar.activation(out=ft[:, j % 2, :], in_=bank,
                                                                           func=AF.Square), [bankr], [fr])
                    k.op("dve", lambda j=j, bank=bank: nc.vector.scalar_tensor_tensor(
                        out=h1T[:, j, :], in0=bank, scalar=0.0, in1=ft[:, j % 2, :], op0=ALU.is_gt, op1=ALU.mult),
                        [bankr, fr], [h1r(j)])
                pend = []
                for oc in range(KC):
                    wbuf = w_get(l, 28 + oc)
                    wv = wbuf.v(BF16)
                    bank, bankr = nextbank()
                    for j in range(64):
                        k.op("pe", lambda j=j, wv=wv, bank=bank: nc.tensor.matmul(
                            bank, lhsT=wv[:, j * 128:(j + 1) * 128], rhs=h1T[:, j, :], start=(j == 0),
                            stop=(j == 63)), [wbuf.reg, h1r(j)], [bankr], inc=(j == 63))
                    resid_evac(l, b, GFA, oc, bank, bankr, pend)
                main_ln(l, b, GFA, A2, B2, "lnfg", "lnfb", not last, pend)
            for tb in range(4):
                st = STG[tb % 2]
                sv = st.v(F32)
                for q in range(4):
                    bank, bankr = nextbank()
                    for j in range(4):
                        kc = q * 4 + j
                        k.op("pe", lambda kc=kc, j=j, bank=bank, tb=tb: nc.tensor.transpose(
                            bank[:, j * 128:(j + 1) * 128], xT[:, kc, tb * 128:(tb + 1) * 128], id32),
                            [xr(kc), ID32.reg], [bankr], inc=(j == 3))
                    k.op("act", lambda q=q, bank=bank, sv=sv: nc.scalar.activation(
                        out=sv[:, q * 512:(q + 1) * 512], in_=bank, func=AF.Copy), [bankr], [st.reg])
                k.dma("act", y_d[tok0 + tb * 128:tok0 + (tb + 1) * 128, :], sv, [st.reg], [R_y], osem[tb % 2])
        for s_ in osem:
            nc.scalar.wait_ge(s_.h, s_.count)
        assert wst["taken"] == len(order)
    return nc


def build_mod_program():
    nc = bass.Bass("TRN2", target_bir_lowering=False)
    nloc_gc = NGC // NCORES
    nloc_grp = nloc_gc // 4
    wmod_d = nc.dram_tensor("wmod", [nloc_grp * 128, 8192], F32, kind="ExternalInput").ap()
    bmod_d = nc.dram_tensor("bmod", [128, nloc_gc], F32, kind="ExternalInput").ap()
    cT_d = nc.dram_tensor("cT", [128, 256], F32, kind="ExternalInput").ap()
    modl_d = nc.dram_tensor("modl", [128, nloc_gc * 16], F32, kind="ExternalOutput").ap()
    Region._all = {}
    es = ExitStack()
    with es:
        k = K(nc, es)
        arena_t = es.enter_context(nc.sbuf_tensor("arena", [128, 70000], U8))
        arena = arena_t[:, :]
        psum_t = es.enter_context(nc.psum_tensor("ps", [128, 512], F32))
        modps = psum_t[:, :]
        bankr = Region("ps", 0, 2048, "bank")
        WMB = [Buf(arena, i * 32768, 32768, f"wmb{i}") for i in range(2)]
        CACT = Buf(arena, 65536, 1024, "cact")
        BMOD = Buf(arena, 66560, nloc_gc * 4, "bmod")
        MODL = Buf(arena, 66688, nloc_gc * 16 * 4, "modl")
        csem = k.newsem("csem")
        osem = k.newsem("osem")
        wmsem = [k.newsem(f"wmsem{i}") for i in range(4)]
        k.dma("sp", CACT.v(F32), cT_d, [], [CACT.reg], csem)
        k.dma("sp", BMOD.v(F32), bmod_d, [], [BMOD.reg], csem)
        CACT.reg.w = (csem, csem.count)
        k.op("act", lambda: nc.scalar.activation(out=CACT.v(F32), in_=CACT.v(F32), func=AF.Silu),
             [CACT.reg], [CACT.reg])
        cact = CACT.v(F32).rearrange("p (k b) -> p k b", b=16)
        for grp in range(nloc_grp):
            wb_ = WMB[grp % 2]
            k.dma("sp", wb_.v(F32), wmod_d[grp * 128:(grp + 1) * 128, :], [], [wb_.reg], wmsem[grp % 2])
            wv = wb_.v(F32)
            for c4 in range(4):
                gcl = grp * 4 + c4
                for kc in range(KC):
                    k.op("pe", lambda c4=c4, kc=kc, gcl=gcl, wv=wv: nc.tensor.matmul(
                        modps[:, gcl * 16:(gcl + 1) * 16],
                        lhsT=wv[:, (c4 * 16 + kc) * 128:(c4 * 16 + kc + 1) * 128],
                        rhs=cact[:, kc, :], start=(kc == 0), stop=(kc == KC - 1)),
                        [wb_.reg, CACT.reg], [bankr], inc=(kc == KC - 1))
        k.op("dve", lambda: nc.vector.tensor_tensor(
            out=MODL.v(F32).rearrange("p (c b) -> p c b", b=16),
            in0=modps[:, 0:nloc_gc * 16].rearrange("p (c b) -> p c b", b=16),
            in1=BMOD.v(F32).unsqueeze(2).to_broadcast([128, nloc_gc, 16]),
            op=ALU.add), [bankr, BMOD.reg], [MODL.reg])
        k.dma("sp", modl_d, MODL.v(F32), [MODL.reg], [], osem)
        nc.sync.wait_ge(osem.h, osem.count)
    return nc


def _chunk_a(W, col):
    return W[:, col:col + 128].reshape(16, 128, 128).transpose(1, 0, 2).reshape(128, 2048)


def _slots_for_layer(w_in, w_out, w_ff1, w_ff2):
    sl = np.zeros((NSLOT, 128, SLOTE), np.float32)
    cols = [128 * g for g in range(4)]
    for c in range(6):
        cols += [1280 + 128 * c, 512 + 128 * c]
    cols += [2048 + 128 * c for c in range(6)]
    for ci, col in enumerate(cols):
        sl[2 + ci // 4][:, (ci % 4) * 2048:(ci % 4 + 1) * 2048] = _chunk_a(w_in, col)
    for half in range(2):
        c0 = 2816 + 384 * half
        sl[half][:, :16 * 384] = w_in[:, c0:c0 + 384].reshape(16, 128, 384).transpose(1, 0, 2).reshape(128, 6144)
    for oc in range(16):
        sl[8 + oc // 4][:, (oc % 4) * 2048:(oc % 4 + 1) * 2048] = _chunk_a(w_out, oc * 128)
    for j in range(64):
        sl[12 + j // 4][:, (j % 4) * 2048:(j % 4 + 1) * 2048] = _chunk_a(w_ff1, j * 128)
    for oc in range(16):
        sl[28 + oc] = w_ff2[:, oc * 128:(oc + 1) * 128].reshape(64, 128, 128).transpose(1, 0, 2).reshape(128, 8192)
    return sl


def _prep_shared(inp):
    f = lambda a: np.ascontiguousarray(np.asarray(a, dtype=np.float32))
    slots = np.concatenate([_slots_for_layer(f(inp["w_in"][l]), f(inp["w_out"][l]), f(inp["w_ff1"][l]),
                                             f(inp["w_ff2"][l])) for l in range(L_ALL)], axis=0)
    slots = slots.reshape(L_ALL * NSLOT * 128, SLOTE)
    w_mod = f(inp["w_mod"])
    wmod = np.concatenate([w_mod[l].reshape(16, 128, 24, 512).transpose(2, 1, 0, 3).reshape(24 * 128, 8192)
                           for l in range(L_ALL)], axis=0)
    bmod = f(inp["b_mod"]).reshape(L_ALL * 96, 128).T.copy()
    cT = f(inp["c"]).T.reshape(16, 128, 16).transpose(1, 0, 2).reshape(128, 256).copy()

    def pch(a, n):
        return f(a).reshape(L_ALL, n, 128).transpose(2, 0, 1)
    convw = f(inp["conv_w"])[:, ::-1, :].reshape(L_ALL, 31, 6, 128).transpose(3, 0, 2, 1)
    parts = [convw.reshape(128, -1), pch(inp["conv_b"], 6).reshape(128, -1), pch(inp["ln_conv_g"], 6).reshape(128, -1),
             pch(inp["ln_conv_b"], 6).reshape(128, -1), pch(inp["ls_pool"], 4).reshape(128, -1),
             pch(inp["ln_mix_g"], 16).reshape(128, -1), pch(inp["ln_mix_b"], 16).reshape(128, -1),
             pch(inp["ln_ff_g"], 16).reshape(128, -1), pch(inp["ln_ff_b"], 16).reshape(128, -1)]
    pp = np.ascontiguousarray(np.concatenate(parts, axis=1))
    sgb = np.concatenate([f(inp["ln_sgu_g"]).reshape(-1), f(inp["ln_sgu_b"]).reshape(-1)])[None, :].copy()
    bsr = f(inp["b_sgu"]).reshape(1, -1).copy()
    wp = f(inp["w_pool"]).transpose(2, 0, 1, 3).reshape(128, -1).copy()
    wst = f(inp["w_sgu"]).transpose(3, 0, 1, 2).reshape(128, -1).copy()
    return dict(slots=slots, wmod=wmod, bmod=bmod, cT=cT, pp=pp, sgb=sgb, bsr=bsr, wp=wp, wst=wst)


def _mod_maps(sh):
    nloc_grp = (NGC // 4) // NCORES
    nloc_gc = NGC // NCORES
    return [{"wmod": sh["wmod"][r * nloc_grp * 128:(r + 1) * nloc_grp * 128],
             "bmod": np.ascontiguousarray(sh["bmod"][:, r * nloc_gc:(r + 1) * nloc_gc]),
             "cT": sh["cT"]} for r in range(NCORES)]


def _mod2_for(modl_list, b0, b1):
    full = np.concatenate([np.asarray(m, dtype=np.float32).reshape(128, NGC // NCORES, 16) for m in modl_list], axis=1)
    return np.ascontiguousarray(full[:, :, [b0, b1]].reshape(128, NGC * 2))


def _main_maps(sh, xs, batches):
    maps = []
    for r in range(len(xs)):
        c2 = np.ascontiguousarray(sh["cT"].reshape(128, 16, 16)[:, :, list(batches[r])].reshape(128, 32))
        maps.append({"x": xs[r], "wsl": sh["slots"], "wmod": sh["wmod"], "bmod": sh["bmod"], "c2": c2,
                     "pp": sh["pp"], "sgb": sh["sgb"], "bsr": sh["bsr"], "wp": sh["wp"], "wst": sh["wst"]})
    return maps


def kernel(**inputs):
    x = np.asarray(inputs["x"], dtype=np.float32)
    B, S_, D_ = x.shape
    sh = _prep_shared(inputs)
    xs = [np.ascontiguousarray(x[2 * r:2 * r + 2].reshape(2 * S_, D_)) for r in range(NCORES)]
    nc = build_program((2 * S_) // T, list(range(L_ALL)))
    res = run_bass_kernel_spmd(nc, _main_maps(sh, xs, [(2 * r, 2 * r + 1) for r in range(NCORES)]),
                               core_ids=list(range(NCORES)))
    out = np.stack([np.asarray(res.results[r]["y"], dtype=np.float32).reshape(2, S_, D_) for r in range(NCORES)])
    return out.reshape(B, S_, D_)
```

```python
import numpy as np
import concourse.bass as bass
import concourse.mybir as mybir
from concourse.bass_utils import run_bass_kernel_spmd
from contextlib import ExitStack

F32 = mybir.dt.float32
BF16 = mybir.dt.bfloat16
U8 = mybir.dt.uint8
I32 = mybir.dt.int32
F32R = mybir.dt.float32r
AF = mybir.ActivationFunctionType
ALU = mybir.AluOpType

D = 2048
KC = 16
T = 512
L_ALL = 2
SEQ = 2048
NSLOT = 44
SLOTE = 8192
ALPHA = (2.0 * L_ALL) ** 0.25
EPS = 1e-5
EPSM = EPS / (ALPHA * ALPHA)
NCORES = 8
NGC = 192


class Sem:
    def __init__(self, h):
        self.h = h
        self.count = 0


class Region:
    _all = {}

    def __init__(self, space, lo, hi, name=""):
        self.space, self.lo, self.hi, self.name = space, lo, hi, name
        self.w = None
        self.r = {}
        self._ov = None
        self._ver = -1
        Region._all.setdefault(space, []).append(self)

    def overlaps(self):
        lst = Region._all[self.space]
        if self._ver != len(lst):
            self._ov = [o for o in lst if o.lo < self.hi and self.lo < o.hi]
            self._ver = len(lst)
        return self._ov


class Eng:
    def __init__(self, h, sem, is_pe=False):
        self.h, self.sem, self.is_pe = h, sem, is_pe
        self.known = {}


class K:
    def __init__(self, nc, es):
        self.nc, self.es = nc, es
        self.eng = {}
        for name, h in (("pe", nc.tensor), ("act", nc.scalar), ("dve", nc.vector),
                        ("pool", nc.gpsimd), ("sp", nc.sync)):
            self.eng[name] = Eng(h, self.newsem("e_" + name), is_pe=(name == "pe"))

    def newsem(self, name):
        return Sem(self.es.enter_context(self.nc.semaphore(name)))

    def _waits(self, E, reads, writes):
        deps = {}

        def add(tk):
            s, v = tk
            if deps.get(s, 0) < v:
                deps[s] = v
        for R in reads:
            for O in R.overlaps():
                if O.w is not None:
                    add(O.w)
        for R in writes:
            for O in R.overlaps():
                if O.w is not None:
                    add(O.w)
                for tk in O.r.items():
                    add(tk)
        for s, v in deps.items():
            if s is E.sem and E.is_pe:
                continue
            if E.known.get(s, 0) >= v:
                continue
            E.h.wait_ge(s.h, v)
            E.known[s] = v

    def op(self, eng, fn, reads=(), writes=(), inc=True):
        E = self.eng[eng]
        self._waits(E, reads, writes)
        ins = fn()
        v = E.sem.count + 1
        if inc:
            ins.then_inc(E.sem.h, 1)
            E.sem.count += 1
        for R in reads:
            if R.r.get(E.sem, 0) < v:
                R.r[E.sem] = v
        for R in writes:
            R.w = (E.sem, v)
            R.r = {}
        return ins

    def dma(self, q, out, in_, reads, writes, sem, **kw):
        E = self.eng[q]
        self._waits(E, reads, writes)
        ins = E.h.dma_start(out=out, in_=in_, **kw)
        ins.then_inc(sem.h, 16)
        sem.count += 16
        for R in reads:
            R.r[sem] = sem.count
        for R in writes:
            R.w = (sem, sem.count)
            R.r = {}
        return ins


class Buf:
    def __init__(self, arena_ap, lo, nbytes, name):
        self.a, self.lo, self.nbytes, self.name = arena_ap, lo, nbytes, name
        self.reg = Region("sb", lo, lo + nbytes, name)
        self._subs = {}

    def v(self, dt, lo=0, n=None):
        esz = 4 if dt in (F32, I32, F32R) else (2 if dt == BF16 else 1)
        if n is None:
            n = (self.nbytes - lo) // esz
        return self.a[:, self.lo + lo:self.lo + lo + n * esz].bitcast(dt)

    def sub(self, lo, nbytes):
        key = (lo, nbytes)
        if key not in self._subs:
            self._subs[key] = Region("sb", self.lo + lo, self.lo + lo + nbytes, f"{self.name}[{lo}]")
        return self._subs[key]


def build_program(nt, layers, stop_after=None):
    n_cores, gather = 1, False
    nc = bass.Bass("TRN2", target_bir_lowering=False)
    ntok = nt * T
    nloc_slots = (L_ALL * NSLOT) // n_cores
    nloc_gc = NGC // n_cores
    nloc_grp = nloc_gc // 4

    def dram(name, shape, dt, kind):
        return nc.dram_tensor(name, shape, dt, kind=kind).ap()
    x_d = dram("x", [ntok, D], F32, "ExternalInput")
    y_d = dram("y", [ntok, D], F32, "ExternalOutput")
    wsl_d = dram("wsl", [nloc_slots * 128, SLOTE], F32, "ExternalInput")
    wmod_d = dram("wmod", [NGC // 4 * 128, 8192], F32, "ExternalInput")
    bmod_d = dram("bmod", [128, NGC], F32, "ExternalInput")
    c2_d = dram("c2", [128, 32], F32, "ExternalInput")
    NPP = L_ALL * (6 * 31 + 6 * 3 + 4 + 16 * 4)
    pp_d = dram("pp", [128, NPP], F32, "ExternalInput")
    sgb_d = dram("sgb", [1, L_ALL * 2 * 768], F32, "ExternalInput")
    bsr_d = dram("bsr", [1, L_ALL * 6 * 128], F32, "ExternalInput")
    wp_d = dram("wp", [128, L_ALL * 4 * 128], F32, "ExternalInput")
    wst_d = dram("wst", [128, L_ALL * 6 * 128], F32, "ExternalInput")
    wbfull_d = dram("wbfull", [L_ALL * NSLOT * 128, SLOTE], BF16, "Internal")
    wbsh_d = wbfull_d

    Region._all = {}
    es = ExitStack()
    with es:
        k = K(nc, es)
        ARENA = 212800
        arena_t = es.enter_context(nc.sbuf_tensor("arena", [128, ARENA], U8))
        arena = arena_t[:, :]
        psum_t = es.enter_context(nc.psum_tensor("ps", [128, 8 * 512], F32))
        psum = psum_t[:, :]
        banks = [psum[:, i * 512:(i + 1) * 512] for i in range(8)]
        bank_r = [Region("ps", i * 2048, (i + 1) * 2048, f"bank{i}") for i in range(8)]
        rr = [0]

        def nextbank():
            i = rr[0] % 6
            rr[0] += 1
            return banks[i], bank_r[i]
        SUMB, SQB = 6, 7

        cur = [0]

        def alloc(name, nbytes, at=None):
            if at is None:
                lo = (cur[0] + 63) // 64 * 64
                cur[0] = lo + nbytes
                assert cur[0] <= ARENA, (name, cur[0])
            else:
                lo = at
            return Buf(arena, lo, nbytes, name)

        XT = alloc("xT", 32768)
        HT = alloc("hT", 16384)
        U0 = (cur[0] + 63) // 64 * 64
        cur[0] = U0 + 65536
        WR = [alloc(f"wr{i}", 16384) for i in range(3)]
        uo = [U0]

        def ualloc(name, nbytes):
            lo = (uo[0] + 63) // 64 * 64
            uo[0] = lo + nbytes
            assert uo[0] <= U0 + 65536, (name, uo[0] - U0)
            return Buf(arena, lo, nbytes, name)
        YT = ualloc("yT", 16384)
        AT = ualloc("aT", 4 * 528 * 4)
        HC = ualloc("hcT", 6 * 544 * 2)
        CV = ualloc("cvT", 6 * 512 * 2)
        UT = ualloc("uT", 6 * 512 * 2)
        VLN = ualloc("vln", 4 * 768 * 2)
        VG = ualloc("vg", 768 * 4)
        PT = ualloc("ptmp", 2 * 528 * 4)
        DG = ualloc("diag", 31 * 128 * 2)
        H1 = Buf(arena, U0, 65536, "h1T")
        DG2 = Buf(arena, AT.lo, 31 * 128 * 2, "diag2")
        STG = [Buf(arena, U0 + i * 8192, 8192, f"stg{i}") for i in range(2)]
        WMB = [Buf(arena, U0 + i * 16384, 16384, f"wmb{i}") for i in range(4)]
        MF = Buf(arena, XT.lo, NGC * 16 * 4, "mf")
        WP32 = Buf(arena, XT.lo + 12288, L_ALL * 4 * 128 * 4, "wp32")
        WST32 = Buf(arena, XT.lo + 16384, L_ALL * 6 * 128 * 4, "wst32")
        ONESF = Buf(arena, XT.lo + 22528, 512, "onesf")
        IOTA = Buf(arena, XT.lo + 23040, 64, "iota")
        IOTF = Buf(arena, XT.lo + 23104, 64, "iotf")
        MSEL = Buf(arena, XT.lo + 23168, NGC * 16 * 4, "msel")
        RB = alloc("rb", 3 * 1024)
        RSQ = alloc("rsq", 3 * 1024)
        MEAN = alloc("mean", 2048)
        RSTD = alloc("rstd", 2048)
        TMPA = alloc("tmpa", 2048)
        FT = alloc("ft", 2 * 2048)
        SQT = alloc("sqt", 2 * 1024)
        ID32 = alloc("id32", 512)
        IDB = alloc("idb", 256)
        ONEB = alloc("oneb", 256)
        ONER = alloc("oner", 512)
        WMT = alloc("wmt", L_ALL * 6 * 128 * 2)
        WPB = alloc("wpb", L_ALL * 4 * 128 * 2)
        SGG = alloc("sgg", L_ALL * 768 * 4)
        SGBB = alloc("sgbb", L_ALL * 768 * 2)
        SGB32 = Buf(arena, XT.lo + 40960, L_ALL * 768 * 4, "sgb32")
        BSR = alloc("bsr", L_ALL * 6 * 128 * 4)
        PP = alloc("pp", NPP * 4)
        MOD2 = alloc("mod2", NGC * 2 * 4)
        DER = alloc("der", 9 * L_ALL * 32 * 4)
        INVC = alloc("invc", 4 * 16 * 4)
        AHALO = alloc("ahalo", L_ALL * 4 * 16 * 4)
        HHALO = alloc("hhalo", L_ALL * 6 * 32 * 2)
        SMALL = alloc("small", 256)
        CACT = Buf(arena, XT.lo + 36864, 256 * 4, "cact")
        SEL = Buf(arena, XT.lo + 39552, 32 * 4, "sel")
        EPSB = alloc("eps", 16)
        if nloc_gc <= 32:
            MODL = Buf(arena, XT.lo + 37888, nloc_gc * 16 * 4, "modl")
            BMOD = Buf(arena, XT.lo + 39424, nloc_gc * 4, "bmod")
        else:
            MODL = Buf(arena, WR[0].lo, NGC * 16 * 4, "modl")
            BMOD = Buf(arena, WR[1].lo, NGC * 4, "bmod")

        ppo = {}
        o = 0
        for nm, n in (("convw", 6 * 31), ("convb", 6), ("lncg", 6), ("lncb", 6), ("ls", 4),
                      ("lnmg", 16), ("lnmb", 16), ("lnfg", 16), ("lnfb", 16)):
            ppo[nm] = (o, n)
            o += L_ALL * n
        assert o == NPP

        def ppv(nm, l, i0=0, n=1):
            o, per = ppo[nm]
            a = o + l * per + i0
            return PP.v(F32)[:, a:a + n]

        csem = k.newsem("csem")
        castsem = k.newsem("castsem")
        gsem = k.newsem("gsem")
        msem = k.newsem("msem")
        wmsem = [k.newsem(f"wmsem{i}") for i in range(4)]
        ringsem = [k.newsem(f"ring{i}") for i in range(3)]
        stgsem = [k.newsem(f"stgs{i}") for i in range(2)]
        osem = [k.newsem(f"osem{i}") for i in range(2)]

        R_wbsh = Region("d_wbsh", 0, 1, "wbsh")
        R_wbfull = Region("d_wbfull", 0, 1, "wbfull") if gather else R_wbsh
        R_modsh = Region("d_modsh", 0, 1, "modsh")
        R_modfull = Region("d_modfull", 0, 1, "modfull") if gather else R_modsh
        R_y = Region("d_y", 0, 1, "y")

        C2 = Buf(arena, XT.lo + 36864, 32 * 4, "c2")
        BMD = Buf(arena, XT.lo + 37888, NGC * 4, "bmd")
        for buf, src in ((PP, pp_d), (C2, c2_d), (BMD, bmod_d), (WP32, wp_d), (WST32, wst_d)):
            k.dma("sp", buf.v(F32), src, [], [buf.reg], csem)
        k.dma("sp", BSR.v(F32), bsr_d.partition_broadcast(128), [], [BSR.reg], csem)
        k.dma("sp", SGG.v(F32), sgb_d[:, 0:L_ALL * 768].partition_broadcast(128), [], [SGG.reg], csem)
        k.dma("sp", SGB32.v(F32), sgb_d[:, L_ALL * 768:2 * L_ALL * 768].partition_broadcast(128), [], [SGB32.reg], csem)
        for b_ in (PP, C2, BMD, WP32, WST32, BSR, SGG, SGB32):
            b_.reg.w = (csem, csem.count)

        R_slot = {g: Region("d_wb", g, g + 1, f"wb{g}") for g in range(L_ALL * NSLOT)}
        stsem = [k.newsem(f"wst{i}") for i in range(3)]
        ringsem_sw = [k.newsem(f"ringsw{i}") for i in range(3)]

        k.op("dve", lambda: nc.vector.memset(EPSB.v(F32)[:, 0:1], EPS), [], [EPSB.reg])
        k.op("dve", lambda: nc.vector.memset(EPSB.v(F32)[:, 1:2], EPSM), [EPSB.reg], [EPSB.reg])
        k.op("pool", lambda: nc.gpsimd.memset(ONESF.v(F32), 1.0), [], [ONESF.reg])
        k.op("pool", lambda: nc.gpsimd.memset(ONEB.v(BF16), 1.0), [], [ONEB.reg])
        k.op("pool", lambda: nc.gpsimd.memset(ONER.v(F32), 1.0), [], [ONER.reg])
        k.op("pool", lambda: nc.gpsimd.affine_select(
            out=ID32.v(F32), in_=ONESF.v(F32), pattern=[[1, 128]], compare_op=ALU.is_equal,
            fill=0.0, base=0, channel_multiplier=-1), [ONESF.reg], [ID32.reg])
        k.op("pool", lambda: nc.gpsimd.tensor_copy(out=IDB.v(BF16), in_=ID32.v(F32)), [ID32.reg], [IDB.reg])
        k.op("pool", lambda: nc.gpsimd.affine_select(
            out=WMT.v(BF16).rearrange("p (a t) -> p a t", t=128),
            in_=WST32.v(F32).rearrange("p (a t) -> p a t", t=128),
            pattern=[[0, L_ALL * 6], [1, 128]], compare_op=ALU.is_ge,
            fill=0.0, base=0, channel_multiplier=-1), [WST32.reg], [WMT.reg])
        k.op("pool", lambda: nc.gpsimd.tensor_copy(out=WPB.v(BF16), in_=WP32.v(F32)), [WP32.reg], [WPB.reg])
        k.op("pool", lambda: nc.gpsimd.tensor_copy(out=SGBB.v(BF16), in_=SGB32.v(F32)), [SGB32.reg], [SGBB.reg])
        k.op("pool", lambda: nc.gpsimd.iota(IOTA.v(I32), pattern=[[1, 16]], base=1, channel_multiplier=0),
             [], [IOTA.reg])
        k.op("dve", lambda: nc.vector.tensor_copy(out=IOTF.v(F32), in_=IOTA.v(I32)), [IOTA.reg], [IOTF.reg])
        for g in range(4):
            k.op("dve", lambda g=g: nc.vector.tensor_scalar(
                out=INVC.v(F32)[:, g * 16:(g + 1) * 16], in0=IOTF.v(F32), scalar1=float(2 << g), scalar2=None,
                op0=ALU.min), [IOTF.reg, INVC.reg], [INVC.reg])
        k.op("dve", lambda: nc.vector.reciprocal(out=INVC.v(F32), in_=INVC.v(F32)), [INVC.reg], [INVC.reg])

        id32 = ID32.v(F32)
        C2B = Buf(arena, XT.lo + 36864 + 128, 64, "c2b")
        k.op("act", lambda: nc.scalar.activation(out=C2B.v(BF16), in_=C2.v(F32), func=AF.Silu), [C2.reg], [C2B.reg])
        c2v = C2B.v(BF16).rearrange("p (k b) -> p k b", b=2)
        modps = banks[SUMB]
        MROW = [Buf(arena, XT.lo + 47104 + i * 2048, 2048, f"mrow{i}") for i in range(2)]
        for grp in range(NGC // 4):
            wb_ = WMB[grp % 4]
            k.dma("pool", wb_.v(BF16), wmod_d[grp * 128:(grp + 1) * 128, :], [], [wb_.reg], wmsem[grp % 4],
                  max_dma_last_dim=8192)
            wv = wb_.v(BF16)
            bank, bankr = nextbank()
            for kc in range(KC):
                k.op("pe", lambda kc=kc, wv=wv, bank=bank: nc.tensor.matmul(
                    bank[0:2, :], lhsT=c2v[:, kc, :], rhs=wv[:, kc * 512:(kc + 1) * 512],
                    start=(kc == 0), stop=(kc == KC - 1)), [wb_.reg, C2B.reg], [bankr], inc=(kc == KC - 1))
            mr = MROW[grp % 2]
            k.op("act", lambda bank=bank, mr=mr: nc.scalar.activation(out=mr.v(F32)[0:2, :], in_=bank[0:2, :],
                                                                     func=AF.Copy), [bankr], [mr.reg])
            for c4 in range(4):
                gc = grp * 4 + c4
                k.op("pe", lambda c4=c4, gc=gc, mr=mr: nc.tensor.matmul(
                    modps[:, gc * 2:(gc + 1) * 2], lhsT=mr.v(F32)[0:2, c4 * 128:(c4 + 1) * 128],
                    rhs=id32[0:2, 0:2], start=True, stop=True), [mr.reg, ID32.reg], [bank_r[SUMB]], inc=(c4 == 3))
        k.op("dve", lambda: nc.vector.tensor_tensor(
            out=MOD2.v(F32).rearrange("p (c b) -> p c b", b=2),
            in0=modps[:, 0:NGC * 2].rearrange("p (c b) -> p c b", b=2),
            in1=BMD.v(F32).unsqueeze(2).to_broadcast([128, NGC, 2]),
            op=ALU.add), [bank_r[SUMB], BMD.reg], [MOD2.reg])
        mod2 = MOD2.v(F32)

        def modv(l, m, kc, b):
            gc = l * 96 + m * 16 + kc
            return mod2[:, gc * 2 + b:gc * 2 + b + 1]

        def mod_all(l, m):
            gc = l * 96 + m * 16
            return mod2[:, gc * 2:(gc + 16) * 2].rearrange("p (k b) -> p k b", b=2)
        der = DER.v(F32)

        def dv(idx, l):
            a = (idx * L_ALL + l) * 32
            return der[:, a:a + 32].rearrange("p (k b) -> p k b", b=2)

        def dsc(idx, l, kc, b):
            a = (idx * L_ALL + l) * 32 + kc * 2 + b
            return der[:, a:a + 1]
        S1M, GMA, GFA, S1F, A1, B1, A2, B2, SHM = range(9)
        DR = [DER.reg, MOD2.reg, PP.reg]

        def ppb(nm, l):
            o, per = ppo[nm]
            return PP.v(F32)[:, o + l * per:o + l * per + 16].unsqueeze(2).to_broadcast([128, 16, 2])
        for l in range(L_ALL):
            k.op("dve", lambda l=l: nc.vector.tensor_scalar(out=dv(S1M, l), in0=mod_all(l, 1), scalar1=1.0,
                                                           scalar2=None, op0=ALU.add), DR, [DER.reg])
            k.op("dve", lambda l=l: nc.vector.tensor_scalar(out=dv(S1F, l), in0=mod_all(l, 4), scalar1=1.0,
                                                           scalar2=None, op0=ALU.add), DR, [DER.reg])
            k.op("dve", lambda l=l: nc.vector.tensor_scalar(out=dv(GMA, l), in0=mod_all(l, 2), scalar1=1.0 / ALPHA,
                                                           scalar2=None, op0=ALU.mult), DR, [DER.reg])
            k.op("dve", lambda l=l: nc.vector.tensor_scalar(out=dv(GFA, l), in0=mod_all(l, 5), scalar1=1.0 / ALPHA,
                                                           scalar2=None, op0=ALU.mult), DR, [DER.reg])
            k.op("dve", lambda l=l: nc.vector.tensor_copy(out=dv(SHM, l), in_=mod_all(l, 0)), DR, [DER.reg])
        for l in range(L_ALL):
            k.op("dve", lambda l=l: nc.vector.tensor_tensor(out=dv(A1, l), in0=dv(S1F, l), in1=ppb("lnmg", l),
                                                           op=ALU.mult), DR, [DER.reg])
            k.op("dve", lambda l=l: nc.vector.tensor_tensor(out=dv(B1, l), in0=dv(S1F, l), in1=ppb("lnmb", l),
                                                           op=ALU.mult), DR, [DER.reg])
            k.op("dve", lambda l=l: nc.vector.tensor_tensor(out=dv(B1, l), in0=dv(B1, l), in1=mod_all(l, 3),
                                                           op=ALU.add), DR, [DER.reg])
            if l + 1 < L_ALL:
                k.op("dve", lambda l=l: nc.vector.tensor_tensor(out=dv(A2, l), in0=dv(S1M, l + 1),
                                                               in1=ppb("lnfg", l), op=ALU.mult), DR, [DER.reg])
                k.op("dve", lambda l=l: nc.vector.tensor_tensor(out=dv(B2, l), in0=dv(S1M, l + 1),
                                                               in1=ppb("lnfb", l), op=ALU.mult), DR, [DER.reg])
                k.op("dve", lambda l=l: nc.vector.tensor_tensor(out=dv(B2, l), in0=dv(B2, l), in1=mod_all(l + 1, 0),
                                                               op=ALU.add), DR, [DER.reg])

        order = [(l, s) for _ in range(nt) for l in layers for s in range(NSLOT)]
        wst = {"issued": 0, "taken": 0}

        def w_issue():
            i = wst["issued"]
            if i >= len(order):
                return
            l, s = order[i]
            g = l * NSLOT + s
            rb_ = WR[i % 3]
            rows = slice(g * 128, (g + 1) * 128)
            if i < NSLOT * len(layers):
                k.dma("pool", rb_.v(BF16), wsl_d[rows, :], [], [rb_.reg], ringsem_sw[i % 3], max_dma_last_dim=8192)
                if nt > 1:
                    k.dma("sp", wbfull_d[rows, :], rb_.v(BF16), [rb_.reg], [R_slot[g]], stsem[i % 3])
            else:
                k.dma("sp", rb_.v(BF16), wbfull_d[rows, :], [R_slot[g]], [rb_.reg], ringsem[i % 3])
            wst["issued"] += 1

        def w_get(l, s, hold=0):
            i = wst["taken"]
            assert order[i] == (l, s), (order[i], l, s)
            while wst["issued"] <= min(i + 2 - hold, len(order) - 1):
                w_issue()
            wst["taken"] += 1
            return WR[i % 3]

        xT = XT.v(F32).rearrange("p (k t) -> p k t", t=T)
        hT = HT.v(BF16).rearrange("p (k t) -> p k t", t=T)
        yT = YT.v(BF16).rearrange("p (k t) -> p k t", t=T)
        h1T = H1.v(BF16).rearrange("p (k t) -> p k t", t=T)
        aT = AT.v(F32).rearrange("p (g t) -> p g t", t=528)
        hcT = HC.v(BF16).rearrange("p (c t) -> p c t", t=544)
        cvT = CV.v(BF16).rearrange("p (c t) -> p c t", t=T)
        uT = UT.v(BF16).rearrange("p (c t) -> p c t", t=T)
        vln = VLN.v(BF16).rearrange("p (b f) -> p b f", f=768)
        vg = VG.v(F32)
        ptmp = PT.v(F32).rearrange("p (i t) -> p i t", t=528)
        diags = [(b_.v(BF16).rearrange("p (k m) -> p k m", m=128), b_.reg) for b_ in (DG, DG2)]
        rb = RB.v(BF16).rearrange("p (i t) -> p i t", t=T)
        rsq = RSQ.v(BF16).rearrange("p (i t) -> p i t", t=T)
        ft = FT.v(F32).rearrange("p (i t) -> p i t", t=T)
        sqt = SQT.v(BF16).rearrange("p (i t) -> p i t", t=T)
        mean, rstd, tmpa = MEAN.v(F32), RSTD.v(F32), TMPA.v(F32)
        id32, idb, oneb = ID32.v(F32), IDB.v(BF16), ONEB.v(BF16)
        wmt = WMT.v(BF16).rearrange("p (a t) -> p a t", t=128)
        wpb = WPB.v(BF16).rearrange("p (a t) -> p a t", t=128)
        sgg = SGG.v(F32).rearrange("p (l f) -> p l f", f=768)
        sgbb = SGBB.v(BF16).rearrange("p (l f) -> p l f", f=768)
        bsr = BSR.v(F32)
        oner = ONER.v(F32)
        ahalo = AHALO.v(F32).rearrange("p (l g t) -> p l g t", g=4, t=16)
        hhalo = HHALO.v(BF16).rearrange("p (l c t) -> p l c t", c=6, t=32)
        small = SMALL.v(F32)
        epsb = EPSB.v(F32)

        def xr(kc):
            return XT.sub(kc * 2048, 2048)

        def hr(kc):
            return HT.sub(kc * 1024, 1024)

        def yr(kc):
            return YT.sub(kc * 1024, 1024)

        def h1r(j):
            return H1.sub(j * 1024, 1024)

        def cvr(c):
            return CV.sub(c * 1024, 1024)

        def ur(c):
            return UT.sub(c * 1024, 1024)

        def hcr(c):
            return HC.sub(c * 1088, 1088)

        def ar(g):
            return AT.sub(g * 2112, 2112)

        def ln_stats(nfeat, eps_col):
            inv = 1.0 / nfeat
            k.op("dve", lambda: nc.vector.tensor_scalar(out=mean, in0=banks[SUMB], scalar1=inv, scalar2=None,
                                                       op0=ALU.mult), [bank_r[SUMB]], [MEAN.reg])
            k.op("dve", lambda: nc.vector.tensor_tensor(out=tmpa, in0=mean, in1=mean, op=ALU.mult),
                 [MEAN.reg], [TMPA.reg])
            k.op("dve", lambda: nc.vector.scalar_tensor_tensor(out=tmpa, in0=banks[SQB], scalar=inv, in1=tmpa,
                                                              op0=ALU.mult, op1=ALU.subtract),
                 [bank_r[SQB], TMPA.reg], [TMPA.reg])
            k.op("act", lambda: nc.scalar.activation(out=rstd, in_=tmpa, func=AF.Sqrt,
                                                    bias=epsb[:, eps_col:eps_col + 1], scale=1.0),
                 [TMPA.reg, EPSB.reg], [RSTD.reg])
            k.op("dve", lambda: nc.vector.reciprocal(out=banks[SQB], in_=rstd), [RSTD.reg], [bank_r[SQB]])

        def main_ln(l, b, gidx, aidx, bidx, gname, bname, make_h, pending):
            for p_ in pending:
                p_()
            ln_stats(float(D), 1)
            for kc in range(KC):
                k.op("dve", lambda kc=kc: nc.vector.scalar_tensor_tensor(
                    out=xT[:, kc, :], in0=banks[SUMB], scalar=-1.0 / D, in1=xT[:, kc, :], op0=ALU.mult, op1=ALU.add),
                    [xr(kc), bank_r[SUMB]], [xr(kc)])
                k.op("dve", lambda kc=kc: nc.vector.tensor_tensor(out=xT[:, kc, :], in0=banks[SQB], in1=xT[:, kc, :],
                                                                 op=ALU.mult), [xr(kc), bank_r[SQB]], [xr(kc)])
                if make_h:
                    k.op("act", lambda kc=kc: nc.scalar.activation(
                        out=hT[:, kc, :], in_=xT[:, kc, :], func=AF.Identity,
                        scale=dsc(aidx, l, kc, b), bias=dsc(bidx, l, kc, b)), [xr(kc), DER.reg], [hr(kc)])
                k.op("act", lambda kc=kc: nc.scalar.activation(
                    out=xT[:, kc, :], in_=xT[:, kc, :], func=AF.Identity,
                    scale=ppv(gname, l, kc), bias=ppv(bname, l, kc)), [xr(kc), PP.reg], [xr(kc)])

        def resid_evac(l, b, gidx, oc, bank, bankr, pend):
            k.op("dve", lambda: nc.vector.scalar_tensor_tensor(
                out=xT[:, oc, :], in0=bank, scalar=dsc(gidx, l, oc, b), in1=xT[:, oc, :],
                op0=ALU.mult, op1=ALU.add), [bankr, xr(oc), DER.reg], [xr(oc)])
            i = oc % 3
            rbr, rsr = RB.sub(i * 1024, 1024), RSQ.sub(i * 1024, 1024)
            k.op("pool", lambda: nc.gpsimd.tensor_copy(out=rb[:, i, :], in_=xT[:, oc, :]), [xr(oc)], [rbr])
            k.op("pool", lambda: nc.gpsimd.tensor_tensor(out=rsq[:, i, :], in0=xT[:, oc, :], in1=xT[:, oc, :],
                                                        op=ALU.mult), [xr(oc)], [rsr])

            def stats():
                k.op("pe", lambda: nc.tensor.matmul(banks[SUMB], lhsT=oneb, rhs=rb[:, i, :], start=(oc == 0),
                                                   stop=(oc == KC - 1)), [ONEB.reg, rbr], [bank_r[SUMB]],
                     inc=(oc == KC - 1))
                k.op("pe", lambda: nc.tensor.matmul(banks[SQB], lhsT=oneb, rhs=rsq[:, i, :], start=(oc == 0),
                                                   stop=(oc == KC - 1)), [ONEB.reg, rsr], [bank_r[SQB]],
                     inc=(oc == KC - 1))
            pend.append(stats)
            if len(pend) > 2:
                pend.pop(0)()

        for ti in range(nt):
            tok0 = ti * T
            b = (tok0 // SEQ) % 2
            seq_start = (tok0 % SEQ == 0)
            for tb in range(4):
                st = STG[tb % 2]
                k.dma("sp", st.v(F32), x_d[tok0 + tb * 128:tok0 + (tb + 1) * 128, :], [], [st.reg], stgsem[tb % 2])
                sv = st.v(F32)
                for q in range(4):
                    bank, bankr = nextbank()
                    for j in range(4):
                        kc = q * 4 + j
                        k.op("pe", lambda kc=kc, j=j, bank=bank, sv=sv: nc.tensor.transpose(
                            bank[:, j * 128:(j + 1) * 128], sv[:, kc * 128:(kc + 1) * 128], id32),
                            [st.reg, ID32.reg], [bankr], inc=(j == 3))
                    k.op("dve", lambda q=q, tb=tb, bank=bank: nc.vector.tensor_copy(
                        out=xT[:, q * 4:(q + 1) * 4, tb * 128:(tb + 1) * 128],
                        in_=bank.rearrange("p (j t) -> p j t", t=128)),
                        [bankr], [xr(q * 4 + j_) for j_ in range(4)])
            for li, l in enumerate(layers):
                last = (li == len(layers) - 1)
                if li == 0:
                    for kc in range(KC):
                        k.op("act", lambda kc=kc: nc.scalar.activation(
                            out=hT[:, kc, :], in_=xT[:, kc, :], func=AF.Identity,
                            scale=dsc(S1M, l, kc, b), bias=dsc(SHM, l, kc, b)), [xr(kc), DER.reg], [hr(kc)])
                if seq_start:
                    k.op("pool", lambda: nc.gpsimd.memset(aT[:, :, 0:16], 0.0), [], [AT.reg])
                    k.op("pool", lambda: nc.gpsimd.memset(hcT[:, :, 0:32], 0.0), [], [HC.reg])
                else:
                    k.op("pool", lambda: nc.gpsimd.tensor_copy(out=aT[:, :, 0:16], in_=ahalo[:, l]),
                         [AHALO.reg], [AT.reg])
                    k.op("pool", lambda: nc.gpsimd.tensor_copy(out=hcT[:, :, 0:32], in_=hhalo[:, l]),
                         [HHALO.reg], [HC.reg])
                w6, w7 = w_get(l, 0), w_get(l, 1, hold=1)
                for tb in range(4):
                    for half, wz in enumerate((w6, w7)):
                        wv = wz.v(BF16)
                        bank, bankr = nextbank()
                        for kc in range(KC):
                            k.op("pe", lambda kc=kc, wv=wv, bank=bank, tb=tb: nc.tensor.matmul(
                                bank[:, 0:384], lhsT=hT[:, kc, tb * 128:(tb + 1) * 128],
                                rhs=wv[:, kc * 384:(kc + 1) * 384], start=(kc == 0), stop=(kc == KC - 1)),
                                [wz.reg, hr(kc)], [bankr], inc=(kc == KC - 1))
                        k.op("act", lambda half=half, bank=bank: nc.scalar.activation(
                            out=vg[:, half * 384:(half + 1) * 384], in_=bank[:, 0:384], func=AF.Gelu),
                            [bankr], [VG.reg])
                    SR = [SMALL.reg]
                    k.op("dve", lambda: nc.vector.bn_stats(out=small[:, 0:6], in_=vg[:, 0:384]), [VG.reg], SR)
                    k.op("dve", lambda: nc.vector.bn_stats(out=small[:, 6:12], in_=vg[:, 384:768]), [VG.reg] + SR, SR)
                    k.op("dve", lambda: nc.vector.bn_aggr(out=small[:, 12:14], in_=small[:, 0:12]), SR, SR)
                    k.op("act", lambda: nc.scalar.activation(out=small[:, 14:15], in_=small[:, 13:14], func=AF.Sqrt,
                                                            bias=epsb[:, 0:1], scale=1.0), SR + [EPSB.reg], SR)
                    k.op("dve", lambda: nc.vector.reciprocal(out=small[:, 15:16], in_=small[:, 14:15]), SR, SR)
                    k.op("dve", lambda: nc.vector.tensor_scalar(
                        out=small[:, 16:17], in0=small[:, 12:13], scalar1=small[:, 15:16], scalar2=-1.0,
                        op0=ALU.mult, op1=ALU.mult), SR, SR)
                    k.op("act", lambda: nc.scalar.activation(out=vg, in_=vg, func=AF.Identity,
                                                            scale=small[:, 15:16], bias=small[:, 16:17]),
                         [VG.reg] + SR, [VG.reg])
                    k.op("dve", lambda: nc.vector.tensor_tensor(out=vg, in0=vg, in1=sgg[:, l, :], op=ALU.mult),
                         [VG.reg, SGG.reg], [VG.reg])
                    k.op("dve", lambda tb=tb: nc.vector.tensor_tensor(out=vln[:, tb, :], in0=vg, in1=sgbb[:, l, :],
                                                                     op=ALU.add),
                         [VG.reg, SGBB.reg], [VLN.sub(tb * 1536, 1536)])
                kinds = [("a", g) for g in range(4)]
                for c in range(6):
                    kinds += [("gate", c), ("val", c)]
                kinds += [("zu", c) for c in range(6)]
                wbuf = None
                for ci, (kind, idx) in enumerate(kinds):
                    if ci % 4 == 0:
                        wbuf = w_get(l, 2 + ci // 4)
                    wv = wbuf.v(BF16)
                    bank, bankr = nextbank()
                    for kc in range(KC):
                        e0 = ((ci % 4) * 16 + kc) * 128
                        k.op("pe", lambda kc=kc, e0=e0, wv=wv, bank=bank: nc.tensor.matmul(
                            bank, lhsT=wv[:, e0:e0 + 128], rhs=hT[:, kc, :], start=(kc == 0), stop=(kc == KC - 1)),
                            [wbuf.reg, hr(kc)], [bankr], inc=(kc == KC - 1))
                    if kind == "gate":
                        fr = FT.sub((idx % 2) * 2048, 2048)
                        k.op("act", lambda idx=idx, bank=bank: nc.scalar.activation(
                            out=ft[:, idx % 2, :], in_=bank, func=AF.Sigmoid), [bankr], [fr])
                    elif kind == "val":
                        fr = FT.sub((idx % 2) * 2048, 2048)
                        k.op("dve", lambda idx=idx, bank=bank: nc.vector.tensor_tensor(
                            out=hcT[:, idx, 32:544], in0=bank, in1=ft[:, idx % 2, :], op=ALU.mult),
                            [bankr, fr], [hcr(idx)])
                    elif kind == "a":
                        k.op("act", lambda idx=idx, bank=bank: nc.scalar.activation(
                            out=aT[:, idx, 16:528], in_=bank, func=AF.Copy), [bankr], [ar(idx)])
                    else:
                        k.op("act", lambda idx=idx, bank=bank: nc.scalar.activation(
                            out=uT[:, idx, :], in_=bank, func=AF.Gelu), [bankr], [ur(idx)])
                for g in range(4):
                    w = 2 << g
                    src, srcr = aT[:, g, :], ar(g)
                    c = 1
                    i = 0
                    while c < w:
                        dst, dstr = ptmp[:, i % 2, :], PT.sub((i % 2) * 2112, 2112)
                        k.op("dve", lambda src=src, dst=dst, c=c: nc.vector.tensor_tensor(
                            out=dst[:, 2 * c - 1:528], in0=src[:, 2 * c - 1:528], in1=src[:, c - 1:528 - c],
                            op=ALU.add), [srcr], [dstr])
                        src, srcr = dst, dstr
                        c *= 2
                        i += 1
                    k.op("dve", lambda src=src, g=g, w=w: nc.vector.scalar_tensor_tensor(
                        out=cvT[:, g, :], in0=src[:, 16:528], scalar=1.0 / w, in1=aT[:, g, 16:528],
                        op0=ALU.mult, op1=ALU.subtract), [srcr, ar(g)], [cvr(g)])
                    if seq_start:
                        k.op("dve", lambda src=src, g=g: nc.vector.tensor_tensor(
                            out=small[:, 32:48], in0=src[:, 16:32], in1=INVC.v(F32)[:, g * 16:(g + 1) * 16],
                            op=ALU.mult), [srcr, INVC.reg], [SMALL.reg])
                        k.op("dve", lambda g=g: nc.vector.tensor_tensor(
                            out=cvT[:, g, 0:16], in0=small[:, 32:48], in1=aT[:, g, 16:32], op=ALU.subtract),
                            [SMALL.reg, ar(g)], [cvr(g)])
                    bank, bankr = nextbank()
                    k.op("pe", lambda g=g, bank=bank: nc.tensor.matmul(
                        bank, lhsT=wpb[:, l * 4 + g, :], rhs=cvT[:, g, :], start=True, stop=True),
                        [WPB.reg, cvr(g)], [bankr])
                    k.op("act", lambda g=g, bank=bank: nc.scalar.activation(
                        out=yT[:, g, :], in_=bank, func=AF.Identity, scale=ppv("ls", l, g), bias=0.0),
                        [bankr, PP.reg], [yr(g)])
                k.op("pool", lambda: nc.gpsimd.tensor_copy(out=ahalo[:, l], in_=aT[:, :, 512:528]),
                     [AT.reg], [AHALO.reg])
                cpend = []
                for c in range(6):
                    diag, dgr = diags[c % 2]
                    k.op("pool", lambda c=c, diag=diag: nc.gpsimd.tensor_tensor(
                        out=diag, in0=idb.unsqueeze(1).to_broadcast([128, 31, 128]),
                        in1=ppv("convw", l, c * 31, 31).unsqueeze(2).to_broadcast([128, 31, 128]),
                        op=ALU.mult), [IDB.reg, PP.reg], [dgr])
                    bank, bankr = nextbank()
                    for kk in range(31):
                        k.op("pe", lambda c=c, kk=kk, bank=bank, diag=diag: nc.tensor.matmul(
                            bank, lhsT=diag[:, kk, :], rhs=hcT[:, c, 32 - kk:544 - kk], start=(kk == 0),
                            stop=(kk == 30)), [dgr, hcr(c)], [bankr], inc=(kk == 30))
                    sr_ = SQT.sub((c % 2) * 1024, 1024)
                    k.op("act", lambda c=c, bank=bank: nc.scalar.activation(
                        out=cvT[:, c, :], in_=bank, func=AF.Identity, bias=ppv("convb", l, c), scale=1.0),
                        [bankr, PP.reg], [cvr(c)])
                    k.op("act", lambda c=c, bank=bank: nc.scalar.activation(
                        out=sqt[:, c % 2, :], in_=bank, func=AF.Square, bias=ppv("convb", l, c), scale=1.0),
                        [bankr, PP.reg], [sr_])

                    def cstats(c=c, sr_=sr_):
                        k.op("pe", lambda: nc.tensor.matmul(banks[SUMB], lhsT=oneb, rhs=cvT[:, c, :],
                                                           start=(c == 0), stop=(c == 5)),
                             [ONEB.reg, cvr(c)], [bank_r[SUMB]], inc=(c == 5))
                        k.op("pe", lambda: nc.tensor.matmul(banks[SQB], lhsT=oneb, rhs=sqt[:, c % 2, :],
                                                           start=(c == 0), stop=(c == 5)),
                             [ONEB.reg, sr_], [bank_r[SQB]], inc=(c == 5))
                    cpend.append(cstats)
                    if len(cpend) > 1:
                        cpend.pop(0)()
                for p_ in cpend:
                    p_()
                k.op("pool", lambda: nc.gpsimd.tensor_copy(out=hhalo[:, l], in_=hcT[:, :, 512:544]),
                     [HC.reg], [HHALO.reg])
                for h in range(6):
                    bank, bankr = nextbank()
                    for tb in range(4):
                        k.op("pe", lambda h=h, tb=tb, bank=bank: nc.tensor.matmul(
                            bank[:, tb * 128:(tb + 1) * 128], lhsT=vln[:, tb, h * 128:(h + 1) * 128],
                            rhs=wmt[:, l * 6 + h, :], start=True, stop=True),
                            [VLN.sub(tb * 1536, 1536), WMT.reg], [bankr], inc=(tb == 3))
                    fr = FT.sub((h % 2) * 2048, 2048)
                    k.op("dve", lambda h=h, bank=bank: nc.vector.tensor_tensor(
                        out=ft[:, h % 2, :].rearrange("p (b t) -> p b t", t=128),
                        in0=bank.rearrange("p (b t) -> p b t", t=128),
                        in1=bsr[:, (l * 6 + h) * 128:(l * 6 + h + 1) * 128].unsqueeze(1).to_broadcast([128, 4, 128]),
                        op=ALU.add), [bankr, BSR.reg], [fr])
                    k.op("dve", lambda h=h: nc.vector.tensor_tensor(
                        out=yT[:, 10 + h, :], in0=ft[:, h % 2, :], in1=uT[:, h, :], op=ALU.mult),
                        [fr, ur(h)], [yr(10 + h)])
                ln_stats(768.0, 0)
                for c in range(6):
                    fr = FT.sub((c % 2) * 2048, 2048)
                    k.op("dve", lambda c=c: nc.vector.scalar_tensor_tensor(
                        out=ft[:, c % 2, :], in0=banks[SUMB], scalar=-1.0 / 768.0, in1=cvT[:, c, :],
                        op0=ALU.mult, op1=ALU.add), [cvr(c), bank_r[SUMB]], [fr])
                    k.op("dve", lambda c=c: nc.vector.tensor_tensor(out=ft[:, c % 2, :], in0=banks[SQB],
                                                                   in1=ft[:, c % 2, :], op=ALU.mult),
                         [fr, bank_r[SQB]], [fr])
                    k.op("act", lambda c=c: nc.scalar.activation(
                        out=yT[:, 4 + c, :], in_=ft[:, c % 2, :], func=AF.Silu,
                        scale=ppv("lncg", l, c), bias=ppv("lncb", l, c)), [fr, PP.reg], [yr(4 + c)])
                if stop_after == 'mixer':
                    return nc
                pend = []
                for oc in range(KC):
                    if oc % 4 == 0:
                        wbuf = w_get(l, 8 + oc // 4)
                    wv = wbuf.v(BF16)
                    bank, bankr = nextbank()
                    korder = [0, 1, 2, 3, 10, 11, 12, 13, 14, 15, 4, 5, 6, 7, 8, 9]
                    for ki, kc in enumerate(korder):
                        e0 = ((oc % 4) * 16 + kc) * 128
                        k.op("pe", lambda kc=kc, ki=ki, e0=e0, wv=wv, bank=bank: nc.tensor.matmul(
                            bank, lhsT=wv[:, e0:e0 + 128], rhs=yT[:, kc, :], start=(ki == 0), stop=(ki == KC - 1)),
                            [wbuf.reg, yr(kc)], [bankr], inc=(ki == KC - 1))
                    resid_evac(l, b, GMA, oc, bank, bankr, pend)
                main_ln(l, b, GMA, A1, B1, "lnmg", "lnmb", True, pend)
                def ffn1_evac(j, bank, bankr):
                    fr = FT.sub((j % 2) * 2048, 2048)
                    k.op("act", lambda: nc.scalar.activation(out=ft[:, j % 2, :], in_=bank, func=AF.Square),
                         [bankr], [fr])
                    k.op("dve", lambda: nc.vector.scalar_tensor_tensor(
                        out=h1T[:, j, :], in0=bank, scalar=0.0, in1=ft[:, j % 2, :], op0=ALU.is_gt, op1=ALU.mult),
                        [bankr, fr], [h1r(j)])
                for j in range(64):
                    if j % 4 == 0:
                        wbuf = w_get(l, 12 + j // 4)
                    wv = wbuf.v(BF16)
                    if j == 0:
                        bks = [nextbank() for _ in range(4)]
                        for kc in range(KC):
                            for jj in range(4):
                                e0 = (jj * 16 + kc) * 128
                                bank, bankr = bks[jj]
                                k.op("pe", lambda kc=kc, e0=e0, wv=wv, bank=bank: nc.tensor.matmul(
                                    bank, lhsT=wv[:, e0:e0 + 128], rhs=hT[:, kc, :], start=(kc == 0),
                                    stop=(kc == KC - 1)), [wbuf.reg, hr(kc)], [bankr], inc=(kc == KC - 1))
                        for jj in range(4):
                            ffn1_evac(jj, *bks[jj])
                        continue
                    if j < 4:
                        continue
                    bank, bankr = nextbank()
                    for kc in range(KC):
                        e0 = ((j % 4) * 16 + kc) * 128
                        k.op("pe", lambda kc=kc, e0=e0, wv=wv, bank=bank: nc.tensor.matmul(
                            bank, lhsT=wv[:, e0:e0 + 128], rhs=hT[:, kc, :], start=(kc == 0), stop=(kc == KC - 1)),
                            [wbuf.reg, hr(kc)], [bankr], inc=(kc == KC - 1))
                    ffn1_evac(j, bank, bankr)
                pend = []
                for oc in range(KC):
                    wbuf = w_get(l, 28 + oc)
                    wv = wbuf.v(BF16)
                    bank, bankr = nextbank()
                    for j in range(64):
                        k.op("pe", lambda j=j, wv=wv, bank=bank: nc.tensor.matmul(
                            bank, lhsT=wv[:, j * 128:(j + 1) * 128], rhs=h1T[:, j, :], start=(j == 0),
                            stop=(j == 63)), [wbuf.reg, h1r(j)], [bankr], inc=(j == 63))
                    resid_evac(l, b, GFA, oc, bank, bankr, pend)
                main_ln(l, b, GFA, A2, B2, "lnfg", "lnfb", not last, pend)
            for tb in range(4):
                st = STG[tb % 2]
                sv = st.v(F32)
                for q in range(4):
                    bank, bankr = nextbank()
                    for j in range(4):
                        kc = q * 4 + j
                        k.op("pe", lambda kc=kc, j=j, bank=bank, tb=tb: nc.tensor.transpose(
                            bank[:, j * 128:(j + 1) * 128], xT[:, kc, tb * 128:(tb + 1) * 128], id32),
                            [xr(kc), ID32.reg], [bankr], inc=(j == 3))
                    k.op("act", lambda q=q, bank=bank, sv=sv: nc.scalar.activation(
                        out=sv[:, q * 512:(q + 1) * 512], in_=bank, func=AF.Copy), [bankr], [st.reg])
                k.dma("act", y_d[tok0 + tb * 128:tok0 + (tb + 1) * 128, :], sv, [st.reg], [R_y], osem[tb % 2])
        for s_ in osem:
            nc.scalar.wait_ge(s_.h, s_.count)
        assert wst["taken"] == len(order)
    return nc


def build_mod_program():
    nc = bass.Bass("TRN2", target_bir_lowering=False)
    nloc_gc = NGC // NCORES
    nloc_grp = nloc_gc // 4
    wmod_d = nc.dram_tensor("wmod", [nloc_grp * 128, 8192], F32, kind="ExternalInput").ap()
    bmod_d = nc.dram_tensor("bmod", [128, nloc_gc], F32, kind="ExternalInput").ap()
    cT_d = nc.dram_tensor("cT", [128, 256], F32, kind="ExternalInput").ap()
    modl_d = nc.dram_tensor("modl", [128, nloc_gc * 16], F32, kind="ExternalOutput").ap()
    Region._all = {}
    es = ExitStack()
    with es:
        k = K(nc, es)
        arena_t = es.enter_context(nc.sbuf_tensor("arena", [128, 70000], U8))
        arena = arena_t[:, :]
        psum_t = es.enter_context(nc.psum_tensor("ps", [128, 512], F32))
        modps = psum_t[:, :]
        bankr = Region("ps", 0, 2048, "bank")
        WMB = [Buf(arena, i * 32768, 32768, f"wmb{i}") for i in range(2)]
        CACT = Buf(arena, 65536, 1024, "cact")
        BMOD = Buf(arena, 66560, nloc_gc * 4, "bmod")
        MODL = Buf(arena, 66688, nloc_gc * 16 * 4, "modl")
        csem = k.newsem("csem")
        osem = k.newsem("osem")
        wmsem = [k.newsem(f"wmsem{i}") for i in range(4)]
        k.dma("sp", CACT.v(F32), cT_d, [], [CACT.reg], csem)
        k.dma("sp", BMOD.v(F32), bmod_d, [], [BMOD.reg], csem)
        CACT.reg.w = (csem, csem.count)
        k.op("act", lambda: nc.scalar.activation(out=CACT.v(F32), in_=CACT.v(F32), func=AF.Silu),
             [CACT.reg], [CACT.reg])
        cact = CACT.v(F32).rearrange("p (k b) -> p k b", b=16)
        for grp in range(nloc_grp):
            wb_ = WMB[grp % 2]
            k.dma("sp", wb_.v(F32), wmod_d[grp * 128:(grp + 1) * 128, :], [], [wb_.reg], wmsem[grp % 2])
            wv = wb_.v(F32)
            for c4 in range(4):
                gcl = grp * 4 + c4
                for kc in range(KC):
                    k.op("pe", lambda c4=c4, kc=kc, gcl=gcl, wv=wv: nc.tensor.matmul(
                        modps[:, gcl * 16:(gcl + 1) * 16],
                        lhsT=wv[:, (c4 * 16 + kc) * 128:(c4 * 16 + kc + 1) * 128],
                        rhs=cact[:, kc, :], start=(kc == 0), stop=(kc == KC - 1)),
                        [wb_.reg, CACT.reg], [bankr], inc=(kc == KC - 1))
        k.op("dve", lambda: nc.vector.tensor_tensor(
            out=MODL.v(F32).rearrange("p (c b) -> p c b", b=16),
            in0=modps[:, 0:nloc_gc * 16].rearrange("p (c b) -> p c b", b=16),
            in1=BMOD.v(F32).unsqueeze(2).to_broadcast([128, nloc_gc, 16]),
            op=ALU.add), [bankr, BMOD.reg], [MODL.reg])
        k.dma("sp", modl_d, MODL.v(F32), [MODL.reg], [], osem)
        nc.sync.wait_ge(osem.h, osem.count)
    return nc


def _chunk_a(W, col):
    return W[:, col:col + 128].reshape(16, 128, 128).transpose(1, 0, 2).reshape(128, 2048)


def _slots_for_layer(w_in, w_out, w_ff1, w_ff2):
    sl = np.zeros((NSLOT, 128, SLOTE), np.float32)
    cols = [128 * g for g in range(4)]
    for c in range(6):
        cols += [1280 + 128 * c, 512 + 128 * c]
    cols += [2048 + 128 * c for c in range(6)]
    for ci, col in enumerate(cols):
        sl[2 + ci // 4][:, (ci % 4) * 2048:(ci % 4 + 1) * 2048] = _chunk_a(w_in, col)
    for half in range(2):
        c0 = 2816 + 384 * half
        sl[half][:, :16 * 384] = w_in[:, c0:c0 + 384].reshape(16, 128, 384).transpose(1, 0, 2).reshape(128, 6144)
    for oc in range(16):
        sl[8 + oc // 4][:, (oc % 4) * 2048:(oc % 4 + 1) * 2048] = _chunk_a(w_out, oc * 128)
    for j in range(64):
        sl[12 + j // 4][:, (j % 4) * 2048:(j % 4 + 1) * 2048] = _chunk_a(w_ff1, j * 128)
    for oc in range(16):
        sl[28 + oc] = w_ff2[:, oc * 128:(oc + 1) * 128].reshape(64, 128, 128).transpose(1, 0, 2).reshape(128, 8192)
    return sl


def _prep_shared(inp):
    f = lambda a: np.ascontiguousarray(np.asarray(a, dtype=np.float32))
    slots = np.concatenate([_slots_for_layer(f(inp["w_in"][l]), f(inp["w_out"][l]), f(inp["w_ff1"][l]),
                                             f(inp["w_ff2"][l])) for l in range(L_ALL)], axis=0)
    slots = slots.reshape(L_ALL * NSLOT * 128, SLOTE)
    w_mod = f(inp["w_mod"])
    wmod = np.concatenate([w_mod[l].reshape(16, 128, 24, 512).transpose(2, 1, 0, 3).reshape(24 * 128, 8192)
                           for l in range(L_ALL)], axis=0)
    bmod = f(inp["b_mod"]).reshape(L_ALL * 96, 128).T.copy()
    cT = f(inp["c"]).T.reshape(16, 128, 16).transpose(1, 0, 2).reshape(128, 256).copy()

    def pch(a, n):
        return f(a).reshape(L_ALL, n, 128).transpose(2, 0, 1)
    convw = f(inp["conv_w"])[:, ::-1, :].reshape(L_ALL, 31, 6, 128).transpose(3, 0, 2, 1)
    parts = [convw.reshape(128, -1), pch(inp["conv_b"], 6).reshape(128, -1), pch(inp["ln_conv_g"], 6).reshape(128, -1),
             pch(inp["ln_conv_b"], 6).reshape(128, -1), pch(inp["ls_pool"], 4).reshape(128, -1),
             pch(inp["ln_mix_g"], 16).reshape(128, -1), pch(inp["ln_mix_b"], 16).reshape(128, -1),
             pch(inp["ln_ff_g"], 16).reshape(128, -1), pch(inp["ln_ff_b"], 16).reshape(128, -1)]
    pp = np.ascontiguousarray(np.concatenate(parts, axis=1))
    sgb = np.concatenate([f(inp["ln_sgu_g"]).reshape(-1), f(inp["ln_sgu_b"]).reshape(-1)])[None, :].copy()
    bsr = f(inp["b_sgu"]).reshape(1, -1).copy()
    wp = f(inp["w_pool"]).transpose(2, 0, 1, 3).reshape(128, -1).copy()
    wst = f(inp["w_sgu"]).transpose(3, 0, 1, 2).reshape(128, -1).copy()
    return dict(slots=slots, wmod=wmod, bmod=bmod, cT=cT, pp=pp, sgb=sgb, bsr=bsr, wp=wp, wst=wst)


def _mod_maps(sh):
    nloc_grp = (NGC // 4) // NCORES
    nloc_gc = NGC // NCORES
    return [{"wmod": sh["wmod"][r * nloc_grp * 128:(r + 1) * nloc_grp * 128],
             "bmod": np.ascontiguousarray(sh["bmod"][:, r * nloc_gc:(r + 1) * nloc_gc]),
             "cT": sh["cT"]} for r in range(NCORES)]


def _mod2_for(modl_list, b0, b1):
    full = np.concatenate([np.asarray(m, dtype=np.float32).reshape(128, NGC // NCORES, 16) for m in modl_list], axis=1)
    return np.ascontiguousarray(full[:, :, [b0, b1]].reshape(128, NGC * 2))


def _main_maps(sh, xs, batches):
    maps = []
    for r in range(len(xs)):
        c2 = np.ascontiguousarray(sh["cT"].reshape(128, 16, 16)[:, :, list(batches[r])].reshape(128, 32))
        maps.append({"x": xs[r], "wsl": sh["slots"], "wmod": sh["wmod"], "bmod": sh["bmod"], "c2": c2,
                     "pp": sh["pp"], "sgb": sh["sgb"], "bsr": sh["bsr"], "wp": sh["wp"], "wst": sh["wst"]})
    return maps


def kernel(**inputs):
    x = np.asarray(inputs["x"], dtype=np.float32)
    B, S_, D_ = x.shape
    sh = _prep_shared(inputs)
    xs = [np.ascontiguousarray(x[2 * r:2 * r + 2].reshape(2 * S_, D_)) for r in range(NCORES)]
    nc = build_program((2 * S_) // T, list(range(L_ALL)))
    res = run_bass_kernel_spmd(nc, _main_maps(sh, xs, [(2 * r, 2 * r + 1) for r in range(NCORES)]),
                               core_ids=list(range(NCORES)))
    out = np.stack([np.asarray(res.results[r]["y"], dtype=np.float32).reshape(2, S_, D_) for r in range(NCORES)])
    return out.reshape(B, S_, D_)
```
